# Optimizing a Trainium2 kernel written in Bass

```python
import jax, jax.numpy as jnp
from jax import lax
import numpy as np

D_MODEL = 1024
BATCH = 8
SEQ = 2048
DEPTH = 4

GRID_W = 64
CTX_LEN = 256
HEAD_DIM = 64
N_HEADS_NA = D_MODEL // (2 * HEAD_DIM)
N_HEADS_GQA = D_MODEL // (2 * HEAD_DIM)
N_KV_GQA = N_HEADS_GQA // 4
W_NA = N_HEADS_NA * HEAD_DIM
W_GQA = N_HEADS_GQA * HEAD_DIM
W_KV = N_KV_GQA * HEAD_DIM
MIX_WIDTH = W_NA + W_GQA
IN_COLS = 3 * W_NA + W_GQA + 2 * W_KV
SPLITS = (W_NA, 2 * W_NA, 3 * W_NA, 3 * W_NA + W_GQA, 3 * W_NA + W_GQA + W_KV)
MAX_WIN_H = 8
WIN_W = 16
Q_BLOCK = 128
ROPE_THETA = 10000.0
D_FF = -(-8 * D_MODEL // (3 * 256)) * 256
EPS = 1e-6

kernel_name = 'hybrid_na_gqa_dit_trunk'


def rms_norm(x, g):
    xf = x.astype(jnp.float32)
    y = xf * lax.rsqrt(jnp.mean(xf * xf, axis=-1, keepdims=True) + EPS)
    return (y * g.astype(jnp.float32)).astype(x.dtype)


def modulate(h, shift, scale):
    return h * (1 + scale) + shift


def adaln_params(cond, w_ada, b_ada):
    m = jax.nn.silu(cond) @ w_ada + b_ada
    return jnp.split(m, 6, axis=-1)


def axial_rope_tables(n_tokens):
    t = jnp.arange(n_tokens, dtype=jnp.int32)
    half = HEAD_DIM // 2
    inv_freq = ROPE_THETA ** (-jnp.arange(0, half, 2, dtype=jnp.float32) / half)
    row = (t // GRID_W).astype(jnp.float32)[:, None] * inv_freq
    col = (t % GRID_W).astype(jnp.float32)[:, None] * inv_freq
    return (jnp.cos(row), jnp.sin(row), jnp.cos(col), jnp.sin(col))


def rotate(x, cos, sin):
    x1, x2 = jnp.split(x, 2, axis=-1)
    cos = cos[:, None, :]
    sin = sin[:, None, :]
    return jnp.concatenate([x1 * cos - x2 * sin, x2 * cos + x1 * sin], axis=-1)


def apply_axial_rope(x, tabs):
    cr, sr, cc, sc = tabs
    xr, xc = jnp.split(x.astype(jnp.float32), 2, axis=-1)
    return jnp.concatenate([rotate(xr, cr, sr), rotate(xc, cc, sc)], axis=-1).astype(x.dtype)


def project_heads(h, w_in, qn_a, kn_a, qn_b, kn_b):
    B, L, _ = h.shape
    p = h @ w_in
    qa, ka, va, qb, kb, vb = jnp.split(p, SPLITS, axis=-1)
    heads = lambda t: t.reshape(B, L, -1, HEAD_DIM)
    return (rms_norm(heads(qa), qn_a), rms_norm(heads(ka), kn_a), heads(va),
            rms_norm(heads(qb), qn_b), rms_norm(heads(kb), kn_b), heads(vb))


def neighbourhood_attention(q, k, v, kc, vc, rpb):
    B, S, H, dh = q.shape
    rows = S // GRID_W
    wh = min(MAX_WIN_H, rows)
    qg = q.reshape(B, rows, GRID_W, H, dh)
    kg = k.reshape(B, rows, GRID_W, H, dh)
    vg = v.reshape(B, rows, GRID_W, H, dh)
    r = jnp.arange(rows, dtype=jnp.int32)
    rs = jnp.clip(r - wh // 2, 0, rows - wh)
    row_idx = rs[:, None] + jnp.arange(wh, dtype=jnp.int32)[None, :]
    k_rows = kg[:, row_idx]
    v_rows = vg[:, row_idx]
    cidx = jnp.arange(GRID_W, dtype=jnp.int32)
    cs = jnp.clip(cidx - WIN_W // 2, 0, GRID_W - WIN_W)
    col_ok = (cidx[None, :] >= cs[:, None]) & (cidx[None, :] < cs[:, None] + WIN_W)
    dr = row_idx - r[:, None] + (MAX_WIN_H - 1)
    dc = jnp.clip(cidx[None, :] - cidx[:, None], -(WIN_W - 1), WIN_W - 1) + (WIN_W - 1)
    bias = rpb[:, dr[:, None, :, None], dc[None, :, None, :]]
    scale = dh ** -0.5
    s_win = jnp.einsum('brqhd,brjkhd->bhrqjk', qg, k_rows,
                       preferred_element_type=jnp.float32) * scale + bias[None].astype(jnp.float32)
    s_win = jnp.where(col_ok[None, None, None, :, None, :], s_win, -jnp.inf)
    s_win = s_win.reshape(B, H, rows, GRID_W, wh * GRID_W)
    s_ctx = jnp.einsum('brqhd,bchd->bhrqc', qg, kc, preferred_element_type=jnp.float32) * scale
    p = jax.nn.softmax(jnp.concatenate([s_win, s_ctx], axis=-1), axis=-1).astype(v.dtype)
    p_win = p[..., :wh * GRID_W].reshape(B, H, rows, GRID_W, wh, GRID_W)
    p_ctx = p[..., wh * GRID_W:]
    o = (jnp.einsum('bhrqjk,brjkhd->brqhd', p_win, v_rows)
         + jnp.einsum('bhrqc,bchd->brqhd', p_ctx, vc))
    return o.reshape(B, S, H * dh)


def gqa_blocked(q, k, v, kc, vc):
    B, S, Hq, dh = q.shape
    Hkv = k.shape[2]
    g = Hq // Hkv
    k_all = jnp.concatenate([k, kc], axis=1)
    v_all = jnp.concatenate([v, vc], axis=1)
    qb = q.reshape(B, S // Q_BLOCK, Q_BLOCK, Hkv, g, dh).transpose(1, 0, 2, 3, 4, 5)
    scale = dh ** -0.5

    def block(q_blk):
        s = jnp.einsum('bqkgd,bskd->bkgqs', q_blk, k_all, preferred_element_type=jnp.float32) * scale
        p = jax.nn.softmax(s, axis=-1).astype(v_all.dtype)
        return jnp.einsum('bkgqs,bskd->bqkgd', p, v_all)

    o = lax.map(block, qb)
    return o.transpose(1, 0, 2, 3, 4, 5).reshape(B, S, Hq * dh)


def context_self_attention(q, k, v):
    B, C, Hq, dh = q.shape
    Hkv = k.shape[2]
    qg = q.reshape(B, C, Hkv, Hq // Hkv, dh)
    s = jnp.einsum('bqkgd,bskd->bkgqs', qg, k, preferred_element_type=jnp.float32) * dh ** -0.5
    p = jax.nn.softmax(s, axis=-1).astype(v.dtype)
    return jnp.einsum('bkgqs,bskd->bqkgd', p, v).reshape(B, C, Hq * dh)


def swiglu(h, w_gate, w_up, w_down):
    return (jax.nn.silu(h @ w_gate) * (h @ w_up)) @ w_down


def setup_inputs(seed: int = 0) -> dict:
    key = jax.random.key(seed)
    ks = jax.random.split(key, 18)
    nrm = lambda k, shape, s: jax.random.normal(k, shape, jnp.float32) * s
    return {
        'x': nrm(ks[0], (BATCH, SEQ, D_MODEL), 1.0),
        'c': nrm(ks[1], (BATCH, D_MODEL), 1.0),
        'ctx': nrm(ks[2], (BATCH, CTX_LEN, D_MODEL), 1.0),
        'c_ctx': nrm(ks[3], (D_MODEL,), 1.0),
        'w_ada': nrm(ks[4], (DEPTH, D_MODEL, 6 * D_MODEL), 0.5 * D_MODEL ** -0.5),
        'b_ada': nrm(ks[5], (DEPTH, 6 * D_MODEL), 0.02),
        'attn_norm': 1.0 + nrm(ks[6], (DEPTH, D_MODEL), 0.02),
        'w_in': nrm(ks[7], (DEPTH, D_MODEL, IN_COLS), D_MODEL ** -0.5),
        'q_norm_a': 1.0 + nrm(ks[8], (DEPTH, HEAD_DIM), 0.02),
        'k_norm_a': 1.0 + nrm(ks[9], (DEPTH, HEAD_DIM), 0.02),
        'q_norm_b': 1.0 + nrm(ks[10], (DEPTH, HEAD_DIM), 0.02),
        'k_norm_b': 1.0 + nrm(ks[11], (DEPTH, HEAD_DIM), 0.02),
        'rpb': nrm(ks[12], (DEPTH, N_HEADS_NA, 2 * MAX_WIN_H - 1, 2 * WIN_W - 1), 0.1),
        'w_out': nrm(ks[13], (DEPTH, MIX_WIDTH, D_MODEL), MIX_WIDTH ** -0.5),
        'ffn_norm': 1.0 + nrm(ks[14], (DEPTH, D_MODEL), 0.02),
        'w_gate': nrm(ks[15], (DEPTH, D_MODEL, D_FF), D_MODEL ** -0.5),
        'w_up': nrm(ks[16], (DEPTH, D_MODEL, D_FF), D_MODEL ** -0.5),
        'w_down': nrm(ks[17], (DEPTH, D_FF, D_MODEL), D_FF ** -0.5),
    }


def reference(x, c, ctx, c_ctx, w_ada, b_ada, attn_norm, w_in, q_norm_a, k_norm_a,
              q_norm_b, k_norm_b, rpb, w_out, ffn_norm, w_gate, w_up, w_down):
    B, S, _ = x.shape
    tabs = axial_rope_tables(S)
    for l in range(DEPTH):
        last = l == DEPTH - 1
        sh1, sc1, g1, sh2, sc2, g2 = [m[:, None, :] for m in adaln_params(c, w_ada[l], b_ada[l])]
        csh1, csc1, cg1, csh2, csc2, cg2 = adaln_params(c_ctx, w_ada[l], b_ada[l])
        h = modulate(rms_norm(x, attn_norm[l]), sh1, sc1)
        hc = modulate(rms_norm(ctx, attn_norm[l]), csh1, csc1)
        qa, ka, va, qb, kb, vb = project_heads(h, w_in[l], q_norm_a[l], k_norm_a[l], q_norm_b[l], k_norm_b[l])
        cqa, cka, cva, cqb, ckb, cvb = project_heads(hc, w_in[l], q_norm_a[l], k_norm_a[l], q_norm_b[l], k_norm_b[l])
        qb = apply_axial_rope(qb, tabs)
        kb = apply_axial_rope(kb, tabs)
        oa = neighbourhood_attention(qa, ka, va, cka, cva, rpb[l])
        ob = gqa_blocked(qb, kb, vb, ckb, cvb)
        x = x + g1 * (jnp.concatenate([oa, ob], axis=-1) @ w_out[l])
        x = x + g2 * swiglu(modulate(rms_norm(x, ffn_norm[l]), sh2, sc2), w_gate[l], w_up[l], w_down[l])
        if not last:
            oc = jnp.concatenate([context_self_attention(cqa, cka, cva),
                                  context_self_attention(cqb, ckb, cvb)], axis=-1)
            ctx = ctx + cg1 * (oc @ w_out[l])
            ctx = ctx + cg2 * swiglu(modulate(rms_norm(ctx, ffn_norm[l]), csh2, csc2),
                                     w_gate[l], w_up[l], w_down[l])
    return x
```

```python
import os
import numpy as np
import concourse.bass as bass
import concourse.mybir as mybir
from concourse.bass_utils import run_bass_kernel_spmd
from contextlib import ExitStack

F32 = mybir.dt.float32
BF16 = mybir.dt.bfloat16
AF = mybir.ActivationFunctionType
ALU = mybir.AluOpType

D = 1024; NT = 2048; NCX = 256; DEPTH = 4; DFF = 2816; NKEY = NT + NCX
EPS = 1e-6
EPOCH = 12000
STOP = int(os.environ.get('KSTOP', '9'))
KSUB = int(os.environ.get('KSUB', '9'))
KNQB = int(os.environ.get('KNQB', '99'))
KD = int(os.environ.get('KD', '9'))
NFI = 14


class T:
    __slots__ = ("name", "w", "rd", "sem", "cnt", "excl")

    def __init__(self, name, excl=False):
        self.name = name; self.w = None; self.rd = []; self.sem = None; self.cnt = 0; self.excl = excl


class Op:
    __slots__ = ("eng", "fn", "deps", "sig", "signal", "is_dma", "sem_tile")


class Sched:
    def __init__(self, nc, stack):
        self.nc = nc; self.stack = stack
        self.ops = {e: [] for e in ("pe", "act", "dve", "pool", "sp")}
        self.final = []
        self.pending = {}
        self.defer = None

    def _deps(self, op, reads, writes):
        ex = [t for t in reads if t.excl]
        if ex:
            writes = list(writes) + [t for t in ex if t not in writes]
        deps = {}
        for t in reads:
            if t.w is not None:
                deps[id(t.w)] = (t.w, True)
        for t in writes:
            if t.w is not None and id(t.w) not in deps:
                deps[id(t.w)] = (t.w, False)
            for r in t.rd:
                if id(r) not in deps:
                    deps[id(r)] = (r, False)
        for p in self.pending.pop(op.eng, ()):
            if id(p) not in deps:
                deps[id(p)] = (p, True)
        keep = []
        for p, raw in deps.values():
            if p is op:
                continue
            if p.eng == op.eng and not p.is_dma and not op.is_dma and not raw:
                continue
            p.sig = True
            keep.append(p)
        op.deps = keep
        for t in reads:
            t.rd.append(op)
        for t in writes:
            t.w = op; t.rd = []

    def capture(self, fn):
        self.defer = []
        fn()
        items = self.defer
        self.defer = None
        return items

    def play(self, item):
        kind, args = item
        if kind == "op":
            self.op(*args)
        else:
            self.dma(*args)

    def op(self, eng, fn, reads=(), writes=()):
        if self.defer is not None:
            self.defer.append(("op", (eng, fn, list(reads), list(writes))))
            return None
        o = Op()
        o.eng = eng; o.fn = fn; o.sig = False; o.signal = None; o.is_dma = False; o.sem_tile = None
        self._deps(o, reads, writes)
        self.ops[eng].append(o)
        return o

    def dma(self, eng, fn, sem_tile, reads=(), writes=()):
        if self.defer is not None:
            self.defer.append(("dma", (eng, fn, sem_tile, list(reads), list(writes))))
            return None
        o = Op()
        o.eng = eng; o.fn = fn; o.sig = True; o.is_dma = True
        if sem_tile.sem is None:
            sem_tile.sem = self.stack.enter_context(self.nc.semaphore("d_" + sem_tile.name))
        sem_tile.cnt += 1
        o.sem_tile = sem_tile
        o.signal = (sem_tile.sem, 16 * sem_tile.cnt, None)
        self._deps(o, reads, writes)
        self.ops[eng].append(o)
        return o

    def barrier(self):
        lasts = [lst[-1] for lst in self.ops.values() if lst]
        for e in self.ops:
            self.pending[e] = list(lasts)

    def emit(self):
        nc = self.nc
        esems = {}
        for e, lst in self.ops.items():
            k = 0
            for o in lst:
                if o.is_dma or not o.sig:
                    continue
                ep = k // EPOCH
                if (e, ep) not in esems:
                    esems[(e, ep)] = self.stack.enter_context(nc.semaphore("e_%s_%d" % (e, ep)))
                o.signal = (esems[(e, ep)], k % EPOCH + 1, (e, k))
                k += 1
        handles = {"pe": "tensor", "act": "scalar", "dve": "vector", "pool": "gpsimd", "sp": "sync"}
        final = self.final

        def make(e, lst):
            def body(eng):
                waited_c = {}
                waited_d = {}
                for o in lst:
                    for p in o.deps:
                        sem, val, key = p.signal
                        if key is not None:
                            if waited_c.get(key[0], -1) >= key[1]:
                                continue
                            waited_c[key[0]] = key[1]
                        else:
                            if waited_d.get(id(sem), 0) >= val:
                                continue
                            waited_d[id(sem)] = val
                        eng.wait_ge(sem, val)
                    ins = o.fn(eng)
                    if o.sig:
                        ins.then_inc(o.signal[0], 16 if o.is_dma else 1)
                if e == "sp":
                    for t in final:
                        if t.sem is not None:
                            eng.wait_ge(t.sem, 16 * t.cnt)
            return body

        with nc.Block() as block:
            for e, lst in self.ops.items():
                if lst or (e == "sp" and final):
                    getattr(block, handles[e])(make(e, lst))


class Buf:
    def __init__(self, base, off, dims):
        self.base = base; self.off = off; self.dims = list(dims)
        n = 1
        for d in dims:
            n *= d
        self.n = n
        v = base[:, off:off + n]
        if len(dims) == 2:
            v = v.rearrange("p (a b) -> p a b", a=dims[0], b=dims[1])
        elif len(dims) == 3:
            v = v.rearrange("p (a b c) -> p a b c", a=dims[0], b=dims[1], c=dims[2])
        self.v = v

    def __getitem__(self, idx):
        return self.v[idx]


def build_program(layers, last_flags, load_name=("xT_in", "cxT_in")):
    nc = bass.Bass("TRN2", target_bir_lowering=False)
    NL = len(layers)
    dram_in = lambda n, s: nc.dram_tensor(n, s, F32, kind="ExternalInput").ap()
    xT_in = dram_in("xT_in", [D, NT]); cxT_in = dram_in("cxT_in", [D, NCX])
    cond_in = dram_in("cond", [128, 16])
    cos_in = dram_in("cosT", [128, NT]); sin_in = dram_in("sinT", [128, NT])
    cmask_in = dram_in("cmask", [128, 64]); perm_in = dram_in("perm", [128, 128])
    W = []
    for i in range(NL):
        W.append(dict(
            wada=dram_in("wada%d" % i, [D, 6 * D]), bada=dram_in("bada%d" % i, [128, 48]),
            norms=dram_in("norms%d" % i, [128, 16]), gains=dram_in("gains%d" % i, [128, 4]),
            win=dram_in("win%d" % i, [D, 2304]), wout=dram_in("wout%d" % i, [D, D]),
            wg=dram_in("wg%d" % i, [D, DFF]), wu=dram_in("wu%d" % i, [D, DFF]),
            wd=dram_in("wd%d" % i, [DFF, D]), rpbg=dram_in("rpbg%d" % i, [128, NFI * 512])))
    xT_out = nc.dram_tensor("xT_out", [D, NT], F32, kind="ExternalOutput").ap()
    cxT_out = nc.dram_tensor("cxT_out", [D, NCX], F32, kind="ExternalOutput").ap()
    scr = nc.dram_tensor("scr", [8, 2, 256], F32).ap()

    with ExitStack() as st:
        S = Sched(nc, st)
        sb = lambda n, s, d: st.enter_context(nc.sbuf_tensor(n, s, d))
        xT = sb("xT", [128, 8, NT], F32); cT = sb("cT", [128, 8, NCX], F32)
        cosT = sb("cosT_s", [128, NT], BF16); sinT = sb("sinT_s", [128, NT], BF16)
        TBf = sb("TBf", [128, NFI, 8, 64], BF16)
        TBs4 = sb("TBs4", [128, 2, 8, 64], BF16); TBm4 = sb("TBm4", [128, 2, 8, 64], BF16)
        cmask = sb("cmask_s", [128, 64], F32)
        ones_b = sb("ones_b", [128, 128], BF16); bones = sb("bones", [128, 128], BF16)
        permb = sb("permb", [128, 128], BF16)
        cond = sb("cond_s", [128, 16], F32); condb = sb("condb", [128, 16], BF16)
        modT = sb("modT", [128, 48, 2], F32); bada = sb("bada_s", [128, 48], F32)
        norms = sb("norms_s", [128, 16], F32); gains = sb("gains_s", [128, 4], F32)
        G1 = sb("G1", [128, 8, 2], F32); G2 = sb("G2", [128, 8, 2], F32)
        BA = sb("BA", [128, 43700], BF16)
        FA = sb("FA", [128, 4352], F32)
        banks = [st.enter_context(nc.psum_tensor("bank%d" % i, [128, 512], F32)) for i in range(8)]
        tbank = [T("bank%d" % i, excl=True) for i in range(8)]

        t_x = [[T("x%d_%d" % (k, b)) for b in range(8)] for k in range(8)]
        t_c = [T("c%d" % k) for k in range(8)]
        t_cos = T("cos"); t_sin = T("sin"); t_tb = T("tb"); t_cm = T("cmask"); t_const = T("const")
        t_perm = T("perm"); t_cond = T("cond"); t_condb = T("condb"); t_mod = T("modT")
        t_bada = T("bada"); t_norms = T("norms"); t_gains = T("gains"); t_G = T("G")
        t_out = T("out")

        o = 0
        KaT = Buf(BA, o, [4, NKEY]); o += 4 * NKEY
        KbT = Buf(BA, o, [NKEY]); o += NKEY
        Va = Buf(BA, o, [18, 4, 129]); o += 18 * 4 * 129
        Vb = Buf(BA, o, [18, 129]); o += 18 * 129
        KV_END = o
        t_ka = T("KaT"); t_kb = T("KbT"); t_va = T("Va"); t_vb = T("Vb")
        def carve(off, specs):
            out = {}
            for n, dims in specs:
                b = Buf(BA, off, dims); out[n] = b; off += b.n
            return out, off
        PA, endA = carve(KV_END, [("hT", [8, 256]), ("hT1", [8, 256]), ("sq2b", [512]), ("sq2c", [512]), ("sqb", [8, 256]), ("Wk", [5, 8, 128]), ("Wv", [8, 640]),
                                  ("sq2", [512]), ("khat", [512])])
        PB, endB = carve(KV_END, [("hT", [8, 256]), ("sqb", [8, 256]), ("Wq0", [8, 128]), ("Wq1", [8, 128]),
                                  ("Wq2", [8, 128]), ("Wq3", [8, 128]), ("sq2", [512]), ("qhat", [512]),
                                  ("Qa0", [4, 256]), ("Qa1", [4, 256]), ("Qb0", [4, 256]), ("Qb1", [4, 256]), ("pt0", [512]), ("pt1", [512]),
                                  ("pt2", [512]), ("pt3", [512]), ("ex0", [512]), ("ex1", [512]), ("oT0", [8, 256]), ("oT1", [8, 256])])
        PC, endC = carve(0, [("h2", [8, 768]), ("h2b", [8, 768]), ("sqb", [8, 256]), ("act", [22, 768]), ("Wg0", [8, 128]),
                             ("Wg1", [8, 128]), ("Wu0", [8, 128]), ("Wu1", [8, 128]), ("Wd0", [22, 128]),
                             ("Wd1", [22, 128])])
        P0, end0 = carve(0, [("wada0", [8, 512]), ("wada1", [8, 512])])
        assert max(endA, endB, endC, end0) <= 43700, (endA, endB, endC, end0)
        def fcarve(specs):
            out = {}; off = 0
            for n, dims in specs:
                b = Buf(FA, off, dims); out[n] = b; off += b.n
            assert off <= 4352, off
            return out
        FAB = fcarve([("rstd", [256]), ("tmp", [2, 256]), ("rs2", [512]), ("t1", [512]), ("t2", [512]),
                      ("RD0", [512]), ("RD1", [512]), ("RB0", [256]), ("RB1", [256]), ("OU0", [256]), ("OU1", [256])])
        FC = fcarve([("rstd", [256]), ("tmp", [2, 256]), ("sg0", [512]), ("sg1", [512])])
        F0 = fcarve([("stage", [1024])])

        S.dma("sp", lambda e: e.dma_start(out=xT[:], in_=xT_in.rearrange("(k p) n -> p k n", p=128)), t_const,
              writes=[t for row in t_x for t in row])
        S.dma("sp", lambda e: e.dma_start(out=cT[:], in_=cxT_in.rearrange("(k p) n -> p k n", p=128)), t_c[0],
              writes=t_c)
        S.dma("pool", lambda e: e.dma_start(out=cosT[:], in_=cos_in), t_cos, writes=[t_cos])
        S.dma("pool", lambda e: e.dma_start(out=sinT[:], in_=sin_in), t_sin, writes=[t_sin])
        S.dma("sp", lambda e: e.dma_start(out=cmask[:], in_=cmask_in), t_cm, writes=[t_cm])
        S.dma("sp", lambda e: e.dma_start(out=cond[:], in_=cond_in), t_cond, writes=[t_cond])
        S.dma("pool", lambda e: e.dma_start(out=permb[:], in_=perm_in), t_perm, writes=[t_perm])
        epsD = sb("epsD", [128, 1], F32); eps64 = sb("eps64", [128, 1], F32)
        def init_const(e):
            e.memset(epsD[:], float(D * EPS))
            e.memset(eps64[:], float(64 * EPS))
            e.memset(ones_b[:], 1.0)
            e.memset(bones[:], 0.0)
            e.memset(bones[0:64, 0:64], 1.0)
            i = e.memset(bones[64:128, 64:128], 1.0)
            return i
        t_ones = T("ones")
        S.op("dve", init_const, writes=[t_ones])
        S.op("act", lambda e: e.activation(out=condb[:], in_=cond[:], func=AF.Silu), reads=[t_cond], writes=[t_condb])

        wq_bufs = ["Wq0", "Wq1", "Wq2", "Wq3"]
        t_named = {}
        def tn(name):
            if name not in t_named:
                t_named[name] = T(name)
            return t_named[name]
        rr = {"wq": 0, "wo": 0, "pt": 0, "ex": 0, "st": 0, "fin": 0, "acc": 0}

        def stream_tiles(s, blk):
            if s == 0:
                return [t_x[k][blk] for k in range(8)]
            return t_c

        def stream_ap(s, k, blk):
            if s == 0:
                return xT[:, k, blk * 256:(blk + 1) * 256]
            return cT[:, k, :]

        def stream_ap3(s, blk):
            if s == 0:
                return xT[:, :, blk * 256:(blk + 1) * 256]
            return cT[:, :, :]

        def norm_mod(s, blk, Gt, shj, hT, sqb, FX, ncols_off=0, heng="act", thT=None, ssr=None):
            st_t = stream_tiles(s, blk)
            if ssr is None:
                ssv = banks[0][:, 0:256]; tss = tbank[0]
            else:
                ssv, tss = ssr
            S.op("act", lambda e: e.activation(out=sqb[:, :, :], in_=stream_ap3(s, blk), func=AF.Square),
                 reads=st_t, writes=[tn("sqb")])
            def mm(e):
                for k in range(8):
                    i = e.matmul(ssv, lhsT=ones_b[:], rhs=sqb[:, k, :], start=(k == 0), stop=(k == 7))
                return i
            S.op("pe", mm, reads=[tn("sqb"), t_ones], writes=[tss])
            S.op("act", lambda e: e.activation(out=FX["rstd"][:, :], in_=ssv, func=AF.Ln, bias=epsD[:, 0:1], scale=1.0),
                 reads=[tss, t_ones], writes=[tn("rstd")])
            S.op("act", lambda e: e.activation(out=FX["rstd"][:, :], in_=FX["rstd"][:, :], func=AF.Exp, scale=-0.5),
                 reads=[tn("rstd")], writes=[tn("rstd")])
            for k in range(8):
                S.op("dve", lambda e, k=k: e.scalar_tensor_tensor(
                    out=FX["tmp"][:, k % 2, :], in0=stream_ap(s, k, blk), scalar=Gt[:, k, s:s + 1],
                    in1=FX["rstd"][:, :], op0=ALU.mult, op1=ALU.mult),
                    reads=[st_t[k], tn("rstd"), t_G], writes=[tn("tmp%d" % (k % 2))])
                S.op("dve", lambda e, k=k: e.tensor_scalar(
                    out=hT[:, k, ncols_off:ncols_off + 256], in0=FX["tmp"][:, k % 2, :],
                    scalar1=modT[:, shj + k, s:s + 1], scalar2=None, op0=ALU.add),
                    reads=[tn("tmp%d" % (k % 2)), t_mod], writes=[thT or tn("hT")])

        def qk_norm(ps, tps, gcol, out_ap, out_tiles, Wd, rope_blk=None, sq2=None, hat=None, R=None):
            if R is None:
                ssb = banks[0]; tss = tbank[0]; sfx = ""
                rs2 = FAB["rs2"]
            else:
                ssb = R["ssb"]; tss = R["tss"]; sfx = R["id"]; rs2 = R["rs2"]
            _tn = tn
            tn_ = lambda n: _tn(n + sfx)
            if Wd == 512:
                v3 = lambda ap: ap.rearrange("p (a b) -> p a b", a=2)
                bc = lambda ap: ap.rearrange("p (a b) -> p a b", a=1).broadcast_to([128, 2, 256])
            else:
                v3 = lambda ap: ap
                bc = lambda ap: ap
            t1 = FAB["t1"]; t2 = FAB["t2"]
            S.op("act", lambda e: e.activation(out=sq2[:, 0:Wd], in_=ps, func=AF.Square), reads=[tps], writes=[tn_("sq2")])
            S.op("pe", lambda e: e.matmul(ssb[:, 0:Wd], lhsT=bones[:], rhs=sq2[:, 0:Wd], start=True, stop=True),
                 reads=[tn_("sq2"), t_ones], writes=[tss])
            S.op("act", lambda e: e.activation(out=rs2[:, 0:Wd], in_=ssb[:, 0:Wd], func=AF.Ln, bias=eps64[:, 0:1], scale=1.0),
                 reads=[tss, t_ones], writes=[tn_("rs2")])
            S.op("act", lambda e: e.activation(out=rs2[:, 0:Wd], in_=rs2[:, 0:Wd], func=AF.Exp, scale=-0.5),
                 reads=[tn_("rs2")], writes=[tn_("rs2")])
            if rope_blk is None:
                S.op("dve", lambda e: e.scalar_tensor_tensor(out=out_ap, in0=v3(ps), scalar=gains[:, gcol:gcol + 1],
                                                             in1=v3(rs2[:, 0:Wd]), op0=ALU.mult, op1=ALU.mult),
                     reads=[tps, tn_("rs2"), t_gains], writes=out_tiles)
                return
            S.op("dve", lambda e: e.scalar_tensor_tensor(out=hat[:, 0:Wd], in0=ps, scalar=gains[:, gcol:gcol + 1],
                                                         in1=rs2[:, 0:Wd], op0=ALU.mult, op1=ALU.mult),
                 reads=[tps, tn_("rs2"), t_gains], writes=[tn_("hat")])
            S.op("pe", lambda e: e.matmul(ssb[:, 0:Wd], lhsT=permb[:], rhs=hat[:, 0:Wd], start=True, stop=True),
                 reads=[tn_("hat"), t_perm], writes=[tss])
            cs = slice(rope_blk * 256, (rope_blk + 1) * 256)
            S.op("dve", lambda e: e.tensor_tensor(out=v3(t1[:, 0:Wd]), in0=v3(ssb[:, 0:Wd]), in1=bc(sinT[:, cs]), op=ALU.mult),
                 reads=[tss, t_sin], writes=[tn_("t1")])
            S.op("dve", lambda e: e.tensor_tensor(out=v3(t2[:, 0:Wd]), in0=v3(hat[:, 0:Wd]), in1=bc(cosT[:, cs]), op=ALU.mult),
                 reads=[tn_("hat"), t_cos], writes=[tn_("t2")])
            S.op("dve", lambda e: e.tensor_tensor(out=out_ap, in0=v3(t1[:, 0:Wd]), in1=v3(t2[:, 0:Wd]), op=ALU.add),
                 reads=[tn_("t1"), tn_("t2")], writes=out_tiles)

        for li in range(NL):
            Wl = W[li]; last = last_flags[li]
            S.barrier()
            S.dma("sp", lambda e, Wl=Wl: e.dma_start(out=bada[:], in_=Wl["bada"]), t_bada, writes=[t_bada])
            S.dma("sp", lambda e, Wl=Wl: e.dma_start(out=norms[:], in_=Wl["norms"]), t_norms, writes=[t_norms])
            S.dma("sp", lambda e, Wl=Wl: e.dma_start(out=gains[:], in_=Wl["gains"]), t_gains, writes=[t_gains])
            mps = banks[1]; tmps = tbank[1]
            for n in range(12):
                wb = P0["wada%d" % (n % 2)]; twb = tn("wada%d" % (n % 2))
                S.dma("pool", lambda e, n=n, wb=wb, Wl=Wl: e.dma_start(
                    out=wb[:, :, :], in_=Wl["wada"][:, n * 512:(n + 1) * 512].rearrange("(k p) n -> p k n", p=128)),
                    twb, writes=[twb])
                def mm(e, n=n, wb=wb):
                    for jj in range(4):
                        j = n * 4 + jj
                        for k in range(8):
                            i = e.matmul(mps[:, 2 * j:2 * j + 2], lhsT=wb[:, k, jj * 128:(jj + 1) * 128],
                                         rhs=condb[:, 2 * k:2 * k + 2], start=(k == 0), stop=(k == 7))
                    return i
                S.op("pe", mm, reads=[twb, t_condb], writes=[tmps])
            for s in range(2):
                S.op("dve", lambda e, s=s: e.tensor_tensor(
                    out=modT[:, :, s], in0=mps[:, 0:96].rearrange("p (j s) -> p j s", s=2)[:, :, s], in1=bada[:, :],
                    op=ALU.add), reads=[tmps, t_bada], writes=[t_mod])
            for s in range(2):
                for (Gt, scj, ncol) in ((G1, 8, 0), (G2, 32, 8)):
                    S.op("dve", lambda e, s=s, Gt=Gt, scj=scj, ncol=ncol: e.scalar_tensor_tensor(
                        out=Gt[:, :, s], in0=modT[:, scj:scj + 8, s], scalar=1.0, in1=norms[:, ncol:ncol + 8],
                        op0=ALU.add, op1=ALU.mult), reads=[t_mod, t_norms], writes=[t_G])
                    S.op("dve", lambda e, s=s, Gt=Gt: e.tensor_scalar(
                        out=Gt[:, :, s], in0=Gt[:, :, s], scalar1=float(np.sqrt(D)), scalar2=None, op0=ALU.mult),
                        reads=[t_G], writes=[t_G])
            S.op("dve", lambda e: e.tensor_scalar(out=gains[:, 1:2], in0=gains[:, 1:2], scalar1=8.0, scalar2=None,
                                                  op0=ALU.mult), reads=[t_gains], writes=[t_gains])
            S.op("dve", lambda e: e.tensor_scalar(out=gains[:, 3:4], in0=gains[:, 3:4], scalar1=8.0, scalar2=None,
                                                  op0=ALU.mult), reads=[t_gains], writes=[t_gains])
            stg = F0["stage"]; tstg = tn("stage")
            for half in range(7):
                S.dma("sp", lambda e, half=half, Wl=Wl: e.dma_start(
                    out=stg[:, :], in_=Wl["rpbg"][:, half * 1024:(half + 1) * 1024]), tstg, writes=[tstg])
                S.op("act", lambda e: e.activation(out=stg[:, :], in_=stg[:, :], func=AF.Exp), reads=[tstg], writes=[tstg])
                S.op("dve", lambda e, half=half: e.tensor_tensor(
                    out=TBf[:, half * 2:(half + 1) * 2, :, :].rearrange("p f h c -> p (f h) c"),
                    in0=stg[:, :].rearrange("p (a c) -> p a c", c=64),
                    in1=cmask[:, :].rearrange("p (a c) -> p a c", a=1).broadcast_to([128, 16, 64]), op=ALU.mult),
                    reads=[tstg, t_cm], writes=[t_tb])
            def specials(e):
                e.memset(TBs4[:, 0, :, :], 0.0)
                e.tensor_copy(out=TBs4[:, 1, :, :], in_=TBf[:, 3, :, :])
                e.tensor_copy(out=TBm4[:, :, :, :], in_=TBf[:, 10:12, :, :])
                e.memset(TBs4[64:128, 1, :, :], 0.0)
                return e.memset(TBm4[0:64, 1, :, :], 0.0)
            S.op("pool", specials, reads=[t_tb], writes=[t_tb])

            if STOP < 1:
                break
            S.barrier()
            hT = PA["hT"]; sqb = PA["sqb"]
            S.op("dve", lambda e: e.memset(Va[:, :, :, 64:65], 1.0), writes=[t_va])
            S.op("dve", lambda e: e.memset(Vb[:, :, 64:65], 1.0), writes=[t_vb])
            for cc in range(4):
                S.dma("pool", lambda e, Wl=Wl, cc=cc: e.dma_start(
                    out=PA["Wk"][:, cc, :, :],
                    in_=Wl["win"][:, 512 + cc * 128:512 + (cc + 1) * 128].rearrange("(k p) n -> p k n", p=128)),
                    tn("Wk"), writes=[tn("Wk")])
            S.dma("pool", lambda e, Wl=Wl: e.dma_start(
                out=PA["Wk"][:, 4, :, :], in_=Wl["win"][:, 2048:2176].rearrange("(k p) n -> p k n", p=128)),
                tn("Wk"), writes=[tn("Wk")])
            S.dma("pool", lambda e, Wl=Wl: e.dma_start(
                out=PA["Wv"][:, :, 0:512], in_=Wl["win"][:, 1024:1536].rearrange("(k p) n -> p k n", p=128)),
                tn("Wv"), writes=[tn("Wv")])
            S.dma("pool", lambda e, Wl=Wl: e.dma_start(
                out=PA["Wv"][:, :, 512:640], in_=Wl["win"][:, 2176:2304].rearrange("(k p) n -> p k n", p=128)),
                tn("Wv"), writes=[tn("Wv")])
            hT2 = [PA["hT"], PA["hT1"]]
            RS = [dict(ssb=banks[3], tss=tbank[3], id="_a0", rs2=FAB["rs2"]),
                  dict(ssb=banks[4], tss=tbank[4], id="_a1", rs2=FAB["RD0"]),
                  dict(ssb=banks[7], tss=tbank[7], id="_a2", rs2=FAB["RD1"])]
            sq2s = [PA["sq2"], PA["sq2b"], PA["sq2c"]]

            def a_norm(tb):
                s = 0 if tb < 8 else 1
                blk = tb if tb < 8 else 0
                norm_mod(s, blk, G1, 0, hT2[tb % 2], sqb, FAB)

            a_norm(0)
            for tb in range(9):
                s = 0 if tb < 8 else 1
                blk = tb if tb < 8 else 0
                k0 = tb * 256
                hT = hT2[tb % 2]
                for pr in range(2):
                    pb = banks[1 + pr]; tpb = tbank[1 + pr]
                    def mm(e, pr=pr, pb=pb, hT=hT):
                        for j in range(2):
                            for k in range(8):
                                i = e.matmul(pb[:, j * 256:(j + 1) * 256], lhsT=PA["Wk"][:, 2 * pr + j, k, :], rhs=hT[:, k, :],
                                             start=(k == 0), stop=(k == 7))
                        return i
                    S.op("pe", mm, reads=[tn("Wk"), tn("hT")], writes=[tpb])
                def mmb(e, hT=hT):
                    for k in range(8):
                        i = e.matmul(banks[0][:, 256:512], lhsT=PA["Wk"][:, 4, k, :], rhs=hT[:, k, :], start=(k == 0), stop=(k == 7))
                    return i
                S.op("pe", mmb, reads=[tn("Wk"), tn("hT")], writes=[tbank[0]])
                for tt in range(2):
                    def mmv(e, tt=tt, hT=hT):
                        for k in range(8):
                            e.matmul(banks[5 + tt][:, 0:512], lhsT=hT[:, k, tt * 128:(tt + 1) * 128], rhs=PA["Wv"][:, k, 0:512],
                                     start=(k == 0), stop=(k == 7))
                        return e
                    def mmv2(e, tt=tt, hT=hT):
                        for k in range(8):
                            i = e.matmul(banks[7][:, 256 + tt * 128:256 + (tt + 1) * 128], lhsT=hT[:, k, tt * 128:(tt + 1) * 128],
                                         rhs=PA["Wv"][:, k, 512:640], start=(k == 0), stop=(k == 7))
                        return i
                    def mmva(e, tt=tt, hT=hT):
                        for k in range(8):
                            i = e.matmul(banks[5 + tt][:, 0:512], lhsT=hT[:, k, tt * 128:(tt + 1) * 128], rhs=PA["Wv"][:, k, 0:512],
                                         start=(k == 0), stop=(k == 7))
                        return i
                    S.op("pe", mmva, reads=[tn("Wv"), tn("hT")], writes=[tbank[5 + tt]])
                    S.op("pe", mmv2, reads=[tn("Wv"), tn("hT")], writes=[tbank[7]])
                streams = []
                for pr in range(2):
                    streams.append(S.capture(lambda pr=pr: qk_norm(
                        banks[1 + pr][:, 0:512], tbank[1 + pr], 1, KaT[:, 2 * pr:2 * pr + 2, k0:k0 + 256], [t_ka], 512,
                        sq2=sq2s[pr], R=RS[pr])))
                streams.append(S.capture(lambda: qk_norm(
                    banks[0][:, 256:512], tbank[0], 3, KbT[:, k0:k0 + 256], [t_kb], 256,
                    rope_blk=(blk if s == 0 else None), sq2=sq2s[2], hat=PA["khat"], R=RS[2])))
                def vev():
                    for tt in range(2):
                        kt = tb * 2 + tt
                        pv = banks[5 + tt][:, 0:512].rearrange("p (c e d) -> p c e d", c=4, e=2)
                        S.op("dve", lambda e, kt=kt, pv=pv: e.tensor_copy(out=Va[:, kt, :, 0:64], in_=pv[:, :, 0, :]),
                             reads=[tbank[5 + tt]], writes=[t_va])
                        S.op("dve", lambda e, kt=kt, pv=pv: e.tensor_copy(out=Va[:, kt, :, 65:129], in_=pv[:, :, 1, :]),
                             reads=[tbank[5 + tt]], writes=[t_va])
                        vb_ps = banks[7][:, 256 + tt * 128:256 + (tt + 1) * 128]
                        S.op("act", lambda e, kt=kt, vb_ps=vb_ps: e.activation(out=Vb[:, kt, 0:64], in_=vb_ps[:, 0:64], func=AF.Copy),
                             reads=[tbank[7]], writes=[t_vb])
                        S.op("act", lambda e, kt=kt, vb_ps=vb_ps: e.activation(out=Vb[:, kt, 65:129], in_=vb_ps[:, 64:128], func=AF.Copy),
                             reads=[tbank[7]], writes=[t_vb])
                streams.append(S.capture(vev))
                if tb + 1 < 9:
                    streams.append(S.capture(lambda: a_norm(tb + 1)))
                while any(streams):
                    for st_ in streams:
                        if st_:
                            S.play(st_.pop(0))

            if STOP < 2:
                break
            S.barrier()
            hT = PB["hT"]; sqb = PB["sqb"]
            cur = {"oT": PB["oT0"], "toT": tn("oT0")}
            delayed = []
            nqb = min(KNQB, 8 if last else 9)
            Qa2 = [PB["Qa0"], PB["Qa1"]]; Qb2 = [PB["Qb0"], PB["Qb1"]]

            def load_w(src_ap, pool="q", Wl=Wl):
                nm = wq_bufs[rr["wq"] % 4]; rr["wq"] += 1
                wbuf = PB[nm]
                S.dma("pool", lambda e, wbuf=wbuf, src_ap=src_ap: e.dma_start(
                    out=wbuf[:, :, :], in_=src_ap.rearrange("(k p) n -> p k n", p=128)), tn(nm), writes=[tn(nm)])
                return wbuf, tn(nm)

            def finalize(chunk, accb, taccb):
                i = rr["fin"] % 2; rr["fin"] += 1
                for d in [d for d in delayed if d[2] == i]:
                    delayed.remove(d); d[1]()
                RD = FAB["RD%d" % i]; RB = FAB["RB%d" % i]; OU = FAB["OU%d" % i]
                tRD = tn("RD%d" % i); tRB = tn("RB%d" % i); tOU = tn("OU%d" % i); tscr = tn("scr%d" % i)
                oT = cur["oT"]; toT = cur["toT"]
                S.op("dve", lambda e: e.reciprocal(out=RD[64:65, 0:256], in_=accb[64:65, 0:256]), reads=[taccb], writes=[tRD])
                S.op("dve", lambda e: e.reciprocal(out=RD[32:64, 256:512], in_=accb[32:64, 256:512]), reads=[taccb], writes=[tRD])
                S.op("dve", lambda e: e.tensor_copy(out=OU[0:64, :], in_=accb[0:64, 0:256]), reads=[taccb], writes=[tOU])
                S.op("dve", lambda e: e.tensor_copy(out=OU[64:128, :], in_=accb[64:128, 256:512]), reads=[taccb], writes=[tOU])
                S.dma("sp", lambda e: e.dma_start(out=scr[i, 1:2, :], in_=RD[64:65, 0:256]), tscr, reads=[tRD], writes=[tscr])
                S.dma("sp", lambda e: e.dma_start(out=scr[i, 0:1, :], in_=RD[63:64, 256:512]), tscr, reads=[tRD], writes=[tscr])
                S.dma("sp", lambda e: e.dma_start(out=RB[0:64, :], in_=scr[i, 1:2, :].partition_broadcast(64)), tRB,
                      reads=[tscr], writes=[tRB])
                S.dma("sp", lambda e: e.dma_start(out=RB[64:128, :], in_=scr[i, 0:1, :].partition_broadcast(64)), tRB,
                      reads=[tscr], writes=[tRB])
                def part2():
                    S.op("dve", lambda e: e.tensor_tensor(out=oT[:, chunk, :], in0=OU[:, :], in1=RB[:, :], op=ALU.mult),
                         reads=[tOU, tRB], writes=[toT])
                delayed.append([8, part2, i])

            def st_banks():
                bi = rr["st"] % 2; rr["st"] += 1
                ix, iy = (2, 4)[bi], (3, 7)[bi]
                return (banks[ix], tbank[ix], banks[iy], tbank[iy], PB["pt%d" % (2 * bi)], tn("pt%d" % (2 * bi)),
                        PB["pt%d" % (2 * bi + 1)], tn("pt%d" % (2 * bi + 1)))

            pend = []
            hops = []

            def play_hops(n):
                for _ in range(n):
                    if hops:
                        S.play(hops.pop(0))

            def emit_step(s1_pe, s1_rest, s2_pe, s2_post, nh=1):
                s1_pe()
                prev = pend.pop() if pend else None
                if prev:
                    prev[0]()
                s1_rest()
                if prev and prev[1]:
                    prev[1]()
                pend.append((s2_pe, s2_post))
                for d in list(delayed):
                    d[0] -= 1
                    if d[0] <= 0:
                        delayed.remove(d); d[1]()
                play_hops(cur.get("nh", nh))

            def flush_steps():
                if pend:
                    p = pend.pop()
                    p[0]()
                    if p[1]:
                        p[1]()
                for d in list(delayed):
                    delayed.remove(d); d[1]()

            def dense_attn(KT_ap, tK, Vfn, tV, Qbuf, c, tQ, ktiles, chunk):
                ai = 5 + rr["acc"] % 2; rr["acc"] += 1
                accb = banks[ai]; taccb = tbank[ai]
                npair = len(ktiles) // 2
                for n_i in range(npair):
                    kt0, kt1 = ktiles[2 * n_i], ktiles[2 * n_i + 1]
                    X, tX, Y, tY, pX, tpX, pY, tpY = st_banks()
                    def mmqk(e, kt0=kt0, kt1=kt1, X=X, Y=Y):
                        e.matmul(X[:, 0:256], lhsT=KT_ap(0, kt0), rhs=Qbuf[0:64, c, :], start=True, stop=True)
                        e.matmul(Y[:, 0:256], lhsT=KT_ap(1, kt0), rhs=Qbuf[64:128, c, :], start=True, stop=True)
                        e.matmul(X[:, 256:512], lhsT=KT_ap(0, kt1), rhs=Qbuf[0:64, c, :], start=True, stop=True)
                        return e.matmul(Y[:, 256:512], lhsT=KT_ap(1, kt1), rhs=Qbuf[64:128, c, :], start=True, stop=True)
                    first = n_i == 0; lastk = n_i == npair - 1
                    def mmpv(e, kt0=kt0, kt1=kt1, pX=pX, pY=pY, first=first, lastk=lastk):
                        e.matmul(accb[:, 0:256], lhsT=Vfn(kt0, 0), rhs=pX[:, 0:256], start=first, stop=False)
                        e.matmul(accb[:, 0:256], lhsT=Vfn(kt1, 0), rhs=pX[:, 256:512], start=False, stop=lastk)
                        e.matmul(accb[:, 256:512], lhsT=Vfn(kt0, 1), rhs=pY[:, 0:256], start=False, stop=False,
                                 skip_group_check=True)
                        return e.matmul(accb[:, 256:512], lhsT=Vfn(kt1, 1), rhs=pY[:, 256:512], start=False, stop=lastk,
                                        skip_group_check=True)
                    def s1_pe(mmqk=mmqk, tX=tX, tY=tY):
                        S.op("pe", mmqk, reads=[tK, tQ], writes=[tX, tY])
                    def s1_rest(X=X, Y=Y, pX=pX, pY=pY, tX=tX, tY=tY, tpX=tpX, tpY=tpY):
                        S.op("act", lambda e: e.activation(out=pX[:, :], in_=X[:, :], func=AF.Exp), reads=[tX], writes=[tpX])
                        S.op("act", lambda e: e.activation(out=pY[:, :], in_=Y[:, :], func=AF.Exp), reads=[tY], writes=[tpY])
                    def s2_pe(mmpv=mmpv, tpX=tpX, tpY=tpY):
                        S.op("pe", mmpv, reads=[tV, tpX, tpY], writes=[taccb])
                    s2_post = (lambda: finalize(chunk, accb, taccb)) if lastk else None
                    emit_step(s1_pe, s1_rest, s2_pe, s2_post)

            KaT_ap = lambda c: (lambda e_, kt: KaT[e_ * 64:(e_ + 1) * 64, c, kt * 128:(kt + 1) * 128])
            Va_fn = lambda c: (lambda kt, e_: (Va[:, kt, c, 0:128] if e_ == 0 else Va[:, kt, c, 1:129]))
            KbT_ap = lambda e_, kt: KbT[e_ * 64:(e_ + 1) * 64, kt * 128:(kt + 1) * 128]
            Vb_fn = lambda kt, e_: (Vb[:, kt, 0:128] if e_ == 0 else Vb[:, kt, 1:129])

            def na_unit(qb, hg, Qa_):
                tQa = tn("Qa%d" % (qb % 2))
                accs = [banks[5], banks[6]]; taccs = [tbank[5], tbank[6]]
                for sub in range(2):
                    r = qb * 4 + sub * 2
                    if r <= 2:
                        tiles = [(kr, "f") for kr in (0, 2, 4, 6)]
                    elif r >= 28:
                        tiles = [(kr, "f") for kr in (24, 26, 28, 30)]
                    else:
                        tiles = [(r + dl, "m") for dl in (-4, -2, 0, 2, 4)]
                    tiles = tiles + [(32, "c"), (34, "c")]
                    for n_i, (kr, kind) in enumerate(tiles):
                        X, tX, Y, tY, pX, tpX, pY, tpY = st_banks()
                        def mmqk(e, kr=kr, X=X, Y=Y, sub=sub):
                            for cl in range(2):
                                c = 2 * hg + cl
                                e.matmul(X[:, cl * 128:(cl + 1) * 128], lhsT=KaT[0:64, c, kr * 64:kr * 64 + 128],
                                         rhs=Qa_[0:64, c, sub * 128:(sub + 1) * 128], start=True, stop=True)
                                i = e.matmul(Y[:, cl * 128:(cl + 1) * 128], lhsT=KaT[64:128, c, kr * 64:kr * 64 + 128],
                                             rhs=Qa_[64:128, c, sub * 128:(sub + 1) * 128], start=True, stop=True)
                            return i
                        def s1_pe(mmqk=mmqk, tX=tX, tY=tY):
                            S.op("pe", mmqk, reads=[t_ka, tQa], writes=[tX, tY])
                        def s1_rest(kind=kind, kr=kr, r=r, X=X, Y=Y, pX=pX, pY=pY, tX=tX, tY=tY, tpX=tpX, tpY=tpY):
                            if kind == "c":
                                S.op("act", lambda e: e.activation(out=pX[:, 0:256], in_=X[:, 0:256], func=AF.Exp),
                                     reads=[tX], writes=[tpX])
                                S.op("act", lambda e: e.activation(out=pY[:, 0:256], in_=Y[:, 0:256], func=AF.Exp),
                                     reads=[tY], writes=[tpY])
                                return
                            dl = kr - r
                            if kind == "m" and dl == 4:
                                tbase, f0 = TBs4, 0
                            elif kind == "m" and dl == -4:
                                tbase, f0 = TBm4, 0
                            else:
                                tbase, f0 = TBf, 6 - dl
                            tv = tbase[:].rearrange("p f (c e) q -> p f c e q", e=2)
                            for e_, (B_, tB_, p_, tp_) in enumerate(((X, tX, pX, tpX), (Y, tY, pY, tpY))):
                                ex = PB["ex%d" % e_]; tex = tn("ex%d" % e_)
                                S.op("act", lambda e, B_=B_, ex=ex: e.activation(out=ex[:, 0:256], in_=B_[:, 0:256], func=AF.Exp),
                                     reads=[tB_], writes=[tex])
                                tab = tv[:, f0:f0 + 2, 2 * hg:2 * hg + 2, e_, :].rearrange("p f c q -> p c f q")
                                S.op("dve", lambda e, ex=ex, p_=p_, tab=tab: e.tensor_tensor(
                                    out=p_[:, 0:256].rearrange("p (c f q) -> p c f q", c=2, f=2),
                                    in0=ex[:, 0:256].rearrange("p (c f q) -> p c f q", c=2, f=2),
                                    in1=tab, op=ALU.mult), reads=[tex, t_tb], writes=[tp_])
                        first = n_i == 0; lastk = n_i == len(tiles) - 1
                        def mmpv(e, kr=kr, pX=pX, pY=pY, first=first, lastk=lastk, sub=sub):
                            for cl in range(2):
                                c = 2 * hg + cl
                                acc = accs[cl]
                                e.matmul(acc[:, sub * 128:(sub + 1) * 128], lhsT=Va[:, kr // 2, c, 0:128],
                                         rhs=pX[:, cl * 128:(cl + 1) * 128], start=first, stop=lastk)
                                i = e.matmul(acc[:, 256 + sub * 128:256 + (sub + 1) * 128], lhsT=Va[:, kr // 2, c, 1:129],
                                             rhs=pY[:, cl * 128:(cl + 1) * 128], start=False, stop=lastk,
                                             skip_group_check=True)
                            return i
                        def s2_pe(mmpv=mmpv, tpX=tpX, tpY=tpY):
                            S.op("pe", mmpv, reads=[t_va, tpX, tpY], writes=taccs)
                        s2_post = None
                        if lastk and sub == 1:
                            s2_post = lambda: [finalize(2 * hg + cl, accs[cl], taccs[cl]) for cl in range(2)]
                        emit_step(s1_pe, s1_rest, s2_pe, s2_post)

            def q_units(qb):
                s = 0 if qb < 8 else 1
                blk = qb if qb < 8 else 0
                par = qb % 2
                units = [lambda: norm_mod(s, blk, G1, 0, hT, sqb, FAB)]
                for pr in range(4):
                    def u(pr=pr):
                        isb = pr >= 2
                        col0 = pr * 256 if not isb else 1536 + (pr - 2) * 256
                        wbufs = [load_w(Wl["win"][:, col0 + j * 128:col0 + (j + 1) * 128]) for j in range(2)]
                        pb = banks[1]; tpb = tbank[1]
                        def mm(e):
                            for j in range(2):
                                for k in range(8):
                                    i = e.matmul(pb[:, j * 256:(j + 1) * 256], lhsT=wbufs[j][0][:, k, :], rhs=hT[:, k, :],
                                                 start=(k == 0), stop=(k == 7))
                            return i
                        S.op("pe", mm, reads=[wbufs[0][1], wbufs[1][1], tn("hT")], writes=[tpb])
                        if not isb:
                            qk_norm(pb[:, 0:512], tpb, 0, Qa2[par][:, 2 * pr:2 * pr + 2, :], [tn("Qa%d" % par)], 512,
                                    sq2=PB["sq2"])
                        else:
                            qk_norm(pb[:, 0:512], tpb, 2, Qb2[par][:, 2 * (pr - 2):2 * (pr - 2) + 2, :], [tn("Qb%d" % par)],
                                    512, rope_blk=(blk if s == 0 else None), sq2=PB["sq2"], hat=PB["qhat"])
                    units.append(u)
                return units

            def att_units(qb):
                par = qb % 2
                Qa_ = Qa2[par]; Qb_ = Qb2[par]; tQa = tn("Qa%d" % par); tQb = tn("Qb%d" % par)
                if qb == 8:
                    ua = [(lambda c=c: dense_attn(KaT_ap(c), t_ka, Va_fn(c), t_va, Qa_, c, tQa, [16, 17], c)) for c in range(4)]
                    ub = [(lambda c=c: dense_attn(KbT_ap, t_kb, Vb_fn, t_vb, Qb_, c, tQb, [16, 17], 4 + c)) for c in range(4)]
                    return ua + ub
                ug = [(lambda c=c: dense_attn(KbT_ap, t_kb, Vb_fn, t_vb, Qb_, c, tQb, list(range(18)), 4 + c)) for c in range(4)]
                un = [(lambda hg=hg: na_unit(qb, hg, Qa_)) for hg in range(2)]
                return [ug[0], un[0], ug[1], ug[2], un[1], ug[3]]

            def wout_unit(qb):
                s = 0 if qb < 8 else 1
                blk = qb if qb < 8 else 0
                oT = PB["oT%d" % (qb % 2)]; toT = tn("oT%d" % (qb % 2))
                st_t = stream_tiles(s, blk)
                for oc in range(8):
                    wbuf, twb = load_w(Wl["wout"][:, oc * 128:(oc + 1) * 128], pool="o")
                    pb = banks[oc % 2]; tpb = tbank[oc % 2]
                    def mm(e, wbuf=wbuf, pb=pb):
                        for k in range(8):
                            i = e.matmul(pb[:, 0:256], lhsT=wbuf[:, k, :], rhs=oT[:, k, :], start=(k == 0), stop=(k == 7))
                        return i
                    S.op("pe", mm, reads=[twb, toT], writes=[tpb])
                    S.op("dve", lambda e, oc=oc, pb=pb: e.scalar_tensor_tensor(
                        out=stream_ap(s, oc, blk), in0=pb[:, 0:256], scalar=modT[:, 16 + oc, s:s + 1],
                        in1=stream_ap(s, oc, blk), op0=ALU.mult, op1=ALU.add),
                        reads=[tpb, t_mod, st_t[oc]], writes=[st_t[oc]])

            for u in q_units(0):
                u()
            for qb in range(nqb):
                cur["oT"] = PB["oT%d" % (qb % 2)]; cur["toT"] = tn("oT%d" % (qb % 2))
                wl = S.capture(lambda: wout_unit(qb - 1)) if qb >= 1 else []
                ql = S.capture(lambda: [u() for u in q_units(qb + 1)]) if qb + 1 < nqb else []
                hops.extend(wl)
                hops.extend(ql)
                nsteps = 64 if qb < 8 else 8
                cur["nh"] = -(-len(hops) // nsteps)
                for u in att_units(qb):
                    u()
                flush_steps()
                play_hops(len(hops))
            wout_unit(nqb - 1)

            if STOP < 3:
                break
            S.barrier()
            h2s = [PC["h2"], PC["h2b"]]; th2 = [tn("h2_0"), tn("h2_1")]
            sqb = PC["sqb"]; act = PC["act"]
            units = [(0, b) for b in range(8)] + ([] if last else [(1, 0)])
            fblocks = [units[i:i + 3] for i in range(0, len(units), 3)]
            def c_norm(ui):
                for j, (s, b) in enumerate(fblocks[ui]):
                    norm_mod(s, b, G2, 24, h2s[ui % 2], sqb, FC, ncols_off=j * 256, thT=th2[ui % 2],
                             ssr=(banks[1][:, 256:512], tbank[1]))
            c_norm(0)
            for ui, ublk in enumerate(fblocks):
                nb = len(ublk)
                halves = [(0, 512), (512, 256)] if nb == 3 else [(0, 512)]
                h2 = h2s[ui % 2]; th2_ = th2[ui % 2]
                nhops = S.capture(lambda: c_norm(ui + 1)) if ui + 1 < len(fblocks) else []
                per_f = -(-len(nhops) // 20)
                for f in range(22):
                    for _ in range(per_f):
                        if nhops:
                            S.play(nhops.pop(0))
                    wg = PC["Wg%d" % (f % 2)]; twg = tn("Wg%d" % (f % 2))
                    wu = PC["Wu%d" % (f % 2)]; twu = tn("Wu%d" % (f % 2))
                    S.dma("pool", lambda e, f=f, wg=wg, Wl=Wl: e.dma_start(
                        out=wg[:, :, :], in_=Wl["wg"][:, f * 128:(f + 1) * 128].rearrange("(k p) n -> p k n", p=128)),
                        twg, writes=[twg])
                    S.dma("pool", lambda e, f=f, wu=wu, Wl=Wl: e.dma_start(
                        out=wu[:, :, :], in_=Wl["wu"][:, f * 128:(f + 1) * 128].rearrange("(k p) n -> p k n", p=128)),
                        twu, writes=[twu])
                    gset = [banks[(f % 2) * 4 + hh] for hh in range(2)]; tg = [tbank[(f % 2) * 4 + hh] for hh in range(2)]
                    uset = [banks[(f % 2) * 4 + 2 + hh] for hh in range(2)]; tu = [tbank[(f % 2) * 4 + 2 + hh] for hh in range(2)]
                    def mmg(e, wg=wg, gset=gset, halves=halves, h2=h2):
                        for k in range(8):
                            for hh, (c0, w) in enumerate(halves):
                                i = e.matmul(gset[hh][:, 0:w], lhsT=wg[:, k, :], rhs=h2[:, k, c0:c0 + w],
                                             start=(k == 0), stop=(k == 7))
                        return i
                    S.op("pe", mmg, reads=[twg, th2_], writes=tg[:len(halves)])
                    def mmu(e, wu=wu, uset=uset, halves=halves, h2=h2):
                        for k in range(8):
                            for hh, (c0, w) in enumerate(halves):
                                i = e.matmul(uset[hh][:, 0:w], lhsT=wu[:, k, :], rhs=h2[:, k, c0:c0 + w],
                                             start=(k == 0), stop=(k == 7))
                        return i
                    S.op("pe", mmu, reads=[twu, th2_], writes=tu[:len(halves)])
                    for hh, (c0, w) in enumerate(halves):
                        sg = FC["sg%d" % hh]; tsg = tn("sg%d" % hh)
                        S.op("act", lambda e, sg=sg, g=gset[hh], w=w: e.activation(out=sg[:, 0:w], in_=g[:, 0:w], func=AF.Silu),
                             reads=[tg[hh]], writes=[tsg])
                        S.op("dve", lambda e, sg=sg, u=uset[hh], f=f, c0=c0, w=w: e.tensor_tensor(
                            out=act[:, f, c0:c0 + w], in0=u[:, 0:w], in1=sg[:, 0:w], op=ALU.mult),
                            reads=[tu[hh], tsg], writes=[tn("act")])
                while nhops:
                    S.play(nhops.pop(0))
                for oc in range(8):
                    wd = PC["Wd%d" % (oc % 2)]; twd = tn("Wd%d" % (oc % 2))
                    S.dma("pool", lambda e, oc=oc, wd=wd, Wl=Wl: e.dma_start(
                        out=wd[:, :, :], in_=Wl["wd"][:, oc * 128:(oc + 1) * 128].rearrange("(f p) n -> p f n", p=128)),
                        twd, writes=[twd])
                    yset = [banks[(oc % 2) * 2 + hh] for hh in range(2)]; ty = [tbank[(oc % 2) * 2 + hh] for hh in range(2)]
                    def mmd(e, wd=wd, yset=yset, halves=halves):
                        for f in range(22):
                            for hh, (c0, w) in enumerate(halves):
                                i = e.matmul(yset[hh][:, 0:w], lhsT=wd[:, f, :], rhs=act[:, f, c0:c0 + w],
                                             start=(f == 0), stop=(f == 21))
                        return i
                    S.op("pe", mmd, reads=[twd, tn("act")], writes=ty[:len(halves)])
                    for hh, (c0, w) in enumerate(halves):
                        for jj in range(w // 256):
                            s, blk = ublk[(c0 + jj * 256) // 256]
                            S.op("dve", lambda e, oc=oc, y=yset[hh], s=s, blk=blk, jj=jj: e.scalar_tensor_tensor(
                                out=stream_ap(s, oc, blk), in0=y[:, jj * 256:(jj + 1) * 256], scalar=modT[:, 40 + oc, s:s + 1],
                                in1=stream_ap(s, oc, blk), op0=ALU.mult, op1=ALU.add),
                                reads=[ty[hh], t_mod, stream_tiles(s, blk)[oc]], writes=[stream_tiles(s, blk)[oc]])

        S.barrier()
        S.dma("sp", lambda e: e.dma_start(out=xT_out.rearrange("(k p) n -> p k n", p=128), in_=xT[:]), t_out,
              reads=[t for row in t_x for t in row])
        S.dma("sp", lambda e: e.dma_start(out=cxT_out.rearrange("(k p) n -> p k n", p=128), in_=cT[:]), t_out,
              reads=t_c)
        S.final.append(t_out)
        S.emit()
    return nc


def _const_tables():
    t = np.arange(NT)
    half = 32
    inv = (10000.0 ** (-np.arange(0, half, 2, dtype=np.float32) / half)).astype(np.float32)
    row = (t // 64).astype(np.float32)[:, None] * inv
    col = (t % 64).astype(np.float32)[:, None] * inv
    cr, sr, cc, sc = np.cos(row), np.sin(row), np.cos(col), np.sin(col)
    cos64 = np.concatenate([cr, cr, cc, cc], axis=1).T
    sin64 = np.concatenate([-sr, sr, -sc, sc], axis=1).T
    cosT = np.concatenate([cos64, cos64], axis=0).astype(np.float32)
    sinT = np.concatenate([sin64, sin64], axis=0).astype(np.float32)
    pi = np.zeros(64, dtype=np.int64)
    for i in range(64):
        base = (i // 32) * 32; j = i % 32
        pi[i] = base + (j + 16) % 32
    perm = np.zeros((128, 128), dtype=np.float32)
    for hh in range(2):
        for i in range(64):
            perm[hh * 64 + pi[i], hh * 64 + i] = 1.0
    cidx = np.arange(64)
    cs = np.clip(cidx - 8, 0, 48)
    col_ok = (cidx[None, :] >= cs[:, None]) & (cidx[None, :] < cs[:, None] + 16)
    cm = col_ok.T.astype(np.float32)
    cmask = np.concatenate([cm, cm], axis=0)
    return np.ascontiguousarray(cosT), np.ascontiguousarray(sinT), perm, np.ascontiguousarray(cmask)


def _layer_arrays(l, w_ada, b_ada, attn_norm, w_in, q_norm_a, k_norm_a, q_norm_b, k_norm_b, rpb, w_out, ffn_norm,
                  w_gate, w_up, w_down):
    fm = lambda v: np.ascontiguousarray(v.reshape(-1, 128).T)
    hb = np.array([c + 4 * e for c in range(4) for e in range(2)])
    colperm = np.arange(2304)
    colperm[1536:2048] = (1536 + hb[:, None] * 64 + np.arange(64)[None, :]).reshape(-1)
    rowperm = np.arange(1024)
    rowperm[512:1024] = (512 + hb[:, None] * 64 + np.arange(64)[None, :]).reshape(-1)
    cidx = np.arange(64)
    dc = np.clip(cidx[:, None] - cidx[None, :], -15, 15) + 15
    g = np.empty((2, 64, NFI, 8, 64), dtype=np.float32)
    for j in range(2):
        for fi in range(NFI):
            dr = 13 + j - fi
            g[j, :, fi, :, :] = np.transpose(rpb[l][:, dr, :][:, dc], (1, 0, 2))
    gains = np.stack([np.tile(q_norm_a[l], 2), np.tile(k_norm_a[l], 2), np.tile(q_norm_b[l], 2), np.tile(k_norm_b[l], 2)], axis=1)
    return dict(
        wada=w_ada[l], bada=fm(b_ada[l]),
        norms=np.ascontiguousarray(np.concatenate([fm(attn_norm[l]), fm(ffn_norm[l])], axis=1)),
        gains=np.ascontiguousarray(gains.astype(np.float32)),
        win=np.ascontiguousarray(w_in[l][:, colperm]), wout=np.ascontiguousarray(w_out[l][rowperm, :]),
        wg=w_gate[l], wu=w_up[l], wd=w_down[l], rpbg=np.ascontiguousarray(g.reshape(128, NFI * 512)))


_PROGS = {}


def _get_prog(nl, last_flags):
    key = (nl, tuple(last_flags))
    if key not in _PROGS:
        _PROGS[key] = build_program(list(range(nl)), list(last_flags))
    return _PROGS[key]


FUSED_LAYERS = 4


def kernel(x, c, ctx, c_ctx, w_ada, b_ada, attn_norm, w_in, q_norm_a, k_norm_a, q_norm_b, k_norm_b, rpb, w_out,
           ffn_norm, w_gate, w_up, w_down):
    f32 = lambda a: np.asarray(a, dtype=np.float32)
    x, c, ctx, c_ctx = f32(x), f32(c), f32(ctx), f32(c_ctx)
    ws = [f32(a) for a in (w_ada, b_ada, attn_norm, w_in, q_norm_a, k_norm_a, q_norm_b, k_norm_b, rpb, w_out,
                           ffn_norm, w_gate, w_up, w_down)]
    B = x.shape[0]
    cosT, sinT, perm, cmask = _const_tables()
    LA = [_layer_arrays(l, *ws) for l in range(DEPTH)]
    xT = [np.ascontiguousarray(x[b].T) for b in range(B)]
    cxT = [np.ascontiguousarray(ctx[b].T) for b in range(B)]
    conds = []
    for b in range(B):
        cd = np.empty((128, 8, 2), dtype=np.float32)
        cd[:, :, 0] = c[b].reshape(8, 128).T
        cd[:, :, 1] = c_ctx.reshape(8, 128).T
        conds.append(np.ascontiguousarray(cd.reshape(128, 16)))
    l = 0
    while l < DEPTH:
        nl = min(FUSED_LAYERS, DEPTH - l)
        flags = [(l + i) == DEPTH - 1 for i in range(nl)]
        nc = _get_prog(nl, flags)
        in_maps = []
        for b in range(B):
            m = {"xT_in": xT[b], "cxT_in": cxT[b], "cond": conds[b], "cosT": cosT, "sinT": sinT, "cmask": cmask,
                 "perm": perm}
            for i in range(nl):
                for k, v in LA[l + i].items():
                    m["%s%d" % (k, i)] = v
            in_maps.append(m)
        res = run_bass_kernel_spmd(nc, in_maps, core_ids=list(range(B)))
        xT = [np.asarray(res.results[b]["xT_out"]) for b in range(B)]
        cxT = [np.asarray(res.results[b]["cxT_out"]) for b in range(B)]
        l += nl
    out = np.stack([xT[b].T for b in range(B)], axis=0).astype(np.float32)
    return out
```

```python
import os
import numpy as np
import concourse.bass as bass
import concourse.mybir as mybir
from concourse.bass_utils import run_bass_kernel_spmd
from contextlib import ExitStack

F32 = mybir.dt.float32
BF16 = mybir.dt.bfloat16
AF = mybir.ActivationFunctionType
ALU = mybir.AluOpType

D = 1024; NT = 2048; NCX = 256; DEPTH = 4; DFF = 2816; NKEY = NT + NCX
EPS = 1e-6
EPOCH = 12000
STOP = int(os.environ.get('KSTOP', '9'))
KSUB = int(os.environ.get('KSUB', '9'))
KNQB = int(os.environ.get('KNQB', '99'))
KD = int(os.environ.get('KD', '9'))
NFI = 14


class T:
    __slots__ = ("name", "w", "rd", "sem", "cnt", "excl")

    def __init__(self, name, excl=False):
        self.name = name; self.w = None; self.rd = []; self.sem = None; self.cnt = 0; self.excl = excl


class Op:
    __slots__ = ("eng", "fn", "deps", "sig", "signal", "is_dma", "sem_tile")


class Sched:
    def __init__(self, nc, stack):
        self.nc = nc; self.stack = stack
        self.ops = {e: [] for e in ("pe", "act", "dve", "pool", "sp")}
        self.final = []
        self.pending = {}
        self.defer = None

    def _deps(self, op, reads, writes):
        ex = [t for t in reads if t.excl]
        if ex:
            writes = list(writes) + [t for t in ex if t not in writes]
        deps = {}
        for t in reads:
            if t.w is not None:
                deps[id(t.w)] = (t.w, True)
        for t in writes:
            if t.w is not None and id(t.w) not in deps:
                deps[id(t.w)] = (t.w, False)
            for r in t.rd:
                if id(r) not in deps:
                    deps[id(r)] = (r, False)
        for p in self.pending.pop(op.eng, ()):
            if id(p) not in deps:
                deps[id(p)] = (p, True)
        keep = []
        for p, raw in deps.values():
            if p is op:
                continue
            if p.eng == op.eng and not p.is_dma and not op.is_dma and not raw:
                continue
            p.sig = True
            keep.append(p)
        op.deps = keep
        for t in reads:
            t.rd.append(op)
        for t in writes:
            t.w = op; t.rd = []

    def capture(self, fn):
        self.defer = []
        fn()
        items = self.defer
        self.defer = None
        return items

    def play(self, item):
        kind, args = item
        if kind == "op":
            self.op(*args)
        else:
            self.dma(*args)

    def op(self, eng, fn, reads=(), writes=()):
        if self.defer is not None:
            self.defer.append(("op", (eng, fn, list(reads), list(writes))))
            return None
        o = Op()
        o.eng = eng; o.fn = fn; o.sig = False; o.signal = None; o.is_dma = False; o.sem_tile = None
        self._deps(o, reads, writes)
        self.ops[eng].append(o)
        return o

    def dma(self, eng, fn, sem_tile, reads=(), writes=()):
        if self.defer is not None:
            self.defer.append(("dma", (eng, fn, sem_tile, list(reads), list(writes))))
            return None
        o = Op()
        o.eng = eng; o.fn = fn; o.sig = True; o.is_dma = True
        if sem_tile.sem is None:
            sem_tile.sem = self.stack.enter_context(self.nc.semaphore("d_" + sem_tile.name))
        sem_tile.cnt += 1
        o.sem_tile = sem_tile
        o.signal = (sem_tile.sem, 16 * sem_tile.cnt, None)
        self._deps(o, reads, writes)
        self.ops[eng].append(o)
        return o

    def barrier(self):
        lasts = [lst[-1] for lst in self.ops.values() if lst]
        for e in self.ops:
            self.pending[e] = list(lasts)

    def emit(self):
        nc = self.nc
        esems = {}
        for e, lst in self.ops.items():
            k = 0
            for o in lst:
                if o.is_dma or not o.sig:
                    continue
                ep = k // EPOCH
                if (e, ep) not in esems:
                    esems[(e, ep)] = self.stack.enter_context(nc.semaphore("e_%s_%d" % (e, ep)))
                o.signal = (esems[(e, ep)], k % EPOCH + 1, (e, k))
                k += 1
        handles = {"pe": "tensor", "act": "scalar", "dve": "vector", "pool": "gpsimd", "sp": "sync"}
        final = self.final

        def make(e, lst):
            def body(eng):
                waited_c = {}
                waited_d = {}
                for o in lst:
                    for p in o.deps:
                        sem, val, key = p.signal
                        if key is not None:
                            if waited_c.get(key[0], -1) >= key[1]:
                                continue
                            waited_c[key[0]] = key[1]
                        else:
                            if waited_d.get(id(sem), 0) >= val:
                                continue
                            waited_d[id(sem)] = val
                        eng.wait_ge(sem, val)
                    ins = o.fn(eng)
                    if o.sig:
                        ins.then_inc(o.signal[0], 16 if o.is_dma else 1)
                if e == "sp":
                    for t in final:
                        if t.sem is not None:
                            eng.wait_ge(t.sem, 16 * t.cnt)
            return body

        with nc.Block() as block:
            for e, lst in self.ops.items():
                if lst or (e == "sp" and final):
                    getattr(block, handles[e])(make(e, lst))


class Buf:
    def __init__(self, base, off, dims):
        self.base = base; self.off = off; self.dims = list(dims)
        n = 1
        for d in dims:
            n *= d
        self.n = n
        v = base[:, off:off + n]
        if len(dims) == 2:
            v = v.rearrange("p (a b) -> p a b", a=dims[0], b=dims[1])
        elif len(dims) == 3:
            v = v.rearrange("p (a b c) -> p a b c", a=dims[0], b=dims[1], c=dims[2])
        self.v = v

    def __getitem__(self, idx):
        return self.v[idx]


def build_program(layers, last_flags, load_name=("xT_in", "cxT_in")):
    nc = bass.Bass("TRN2", target_bir_lowering=False)
    NL = len(layers)
    dram_in = lambda n, s: nc.dram_tensor(n, s, F32, kind="ExternalInput").ap()
    xT_in = dram_in("xT_in", [D, NT]); cxT_in = dram_in("cxT_in", [D, NCX])
    cond_in = dram_in("cond", [128, 16])
    cos_in = dram_in("cosT", [128, NT]); sin_in = dram_in("sinT", [128, NT])
    cmask_in = dram_in("cmask", [128, 64]); perm_in = dram_in("perm", [128, 128])
    W = []
    for i in range(NL):
        W.append(dict(
            wada=dram_in("wada%d" % i, [D, 6 * D]), bada=dram_in("bada%d" % i, [128, 48]),
            norms=dram_in("norms%d" % i, [128, 16]), gains=dram_in("gains%d" % i, [128, 4]),
            win=dram_in("win%d" % i, [D, 2304]), wout=dram_in("wout%d" % i, [D, D]),
            wg=dram_in("wg%d" % i, [D, DFF]), wu=dram_in("wu%d" % i, [D, DFF]),
            wd=dram_in("wd%d" % i, [DFF, D]), rpbg=dram_in("rpbg%d" % i, [128, NFI * 512])))
    xT_out = nc.dram_tensor("xT_out", [D, NT], F32, kind="ExternalOutput").ap()
    cxT_out = nc.dram_tensor("cxT_out", [D, NCX], F32, kind="ExternalOutput").ap()
    scr = nc.dram_tensor("scr", [8, 2, 256], F32).ap()

    with ExitStack() as st:
        S = Sched(nc, st)
        sb = lambda n, s, d: st.enter_context(nc.sbuf_tensor(n, s, d))
        xT = sb("xT", [128, 8, NT], F32); cT = sb("cT", [128, 8, NCX], F32)
        cosT = sb("cosT_s", [128, NT], BF16); sinT = sb("sinT_s", [128, NT], BF16)
        TBf = sb("TBf", [128, NFI, 8, 64], BF16)
        TBs4 = sb("TBs4", [128, 2, 8, 64], BF16); TBm4 = sb("TBm4", [128, 2, 8, 64], BF16)
        cmask = sb("cmask_s", [128, 64], F32)
        ones_b = sb("ones_b", [128, 128], BF16); bones = sb("bones", [128, 128], BF16)
        permb = sb("permb", [128, 128], BF16)
        cond = sb("cond_s", [128, 16], F32); condb = sb("condb", [128, 16], BF16)
        modT = sb("modT", [128, 48, 2], F32); bada = sb("bada_s", [128, 48], F32)
        norms = sb("norms_s", [128, 16], F32); gains = sb("gains_s", [128, 4], F32)
        G1 = sb("G1", [128, 8, 2], F32); G2 = sb("G2", [128, 8, 2], F32)
        BA = sb("BA", [128, 43700], BF16)
        FA = sb("FA", [128, 4352], F32)
        banks = [st.enter_context(nc.psum_tensor("bank%d" % i, [128, 512], F32)) for i in range(8)]
        tbank = [T("bank%d" % i, excl=True) for i in range(8)]

        t_x = [[T("x%d_%d" % (k, b)) for b in range(8)] for k in range(8)]
        t_c = [T("c%d" % k) for k in range(8)]
        t_cos = T("cos"); t_sin = T("sin"); t_tb = T("tb"); t_cm = T("cmask"); t_const = T("const")
        t_perm = T("perm"); t_cond = T("cond"); t_condb = T("condb"); t_mod = T("modT")
        t_bada = T("bada"); t_norms = T("norms"); t_gains = T("gains"); t_G = T("G")
        t_out = T("out")

        o = 0
        KaT = Buf(BA, o, [4, NKEY]); o += 4 * NKEY
        KbT = Buf(BA, o, [NKEY]); o += NKEY
        Va = Buf(BA, o, [18, 4, 129]); o += 18 * 4 * 129
        Vb = Buf(BA, o, [18, 129]); o += 18 * 129
        KV_END = o
        t_ka = T("KaT"); t_kb = T("KbT"); t_va = T("Va"); t_vb = T("Vb")
        def carve(off, specs):
            out = {}
            for n, dims in specs:
                b = Buf(BA, off, dims); out[n] = b; off += b.n
            return out, off
        PA, endA = carve(KV_END, [("hT", [8, 256]), ("hT1", [8, 256]), ("sq2b", [512]), ("sq2c", [512]), ("sqb", [8, 256]), ("Wk", [5, 8, 128]), ("Wv", [8, 640]),
                                  ("sq2", [512]), ("khat", [512])])
        PB, endB = carve(KV_END, [("hT", [8, 256]), ("sqb", [8, 256]), ("Wq0", [8, 128]), ("Wq1", [8, 128]),
                                  ("Wq2", [8, 128]), ("Wq3", [8, 128]), ("sq2", [512]), ("qhat", [512]),
                                  ("Qa0", [4, 256]), ("Qa1", [4, 256]), ("Qb0", [4, 256]), ("Qb1", [4, 256]), ("pt0", [512]), ("pt1", [512]),
                                  ("pt2", [512]), ("pt3", [512]), ("ex0", [512]), ("ex1", [512]), ("oT0", [8, 256]), ("oT1", [8, 256])])
        PC, endC = carve(0, [("h2", [8, 768]), ("h2b", [8, 768]), ("sqb", [8, 256]), ("act", [22, 768]), ("Wg0", [8, 128]),
                             ("Wg1", [8, 128]), ("Wu0", [8, 128]), ("Wu1", [8, 128]), ("Wd0", [22, 128]),
                             ("Wd1", [22, 128])])
        P0, end0 = carve(0, [("wada0", [8, 512]), ("wada1", [8, 512])])
        assert max(endA, endB, endC, end0) <= 43700, (endA, endB, endC, end0)
        def fcarve(specs):
            out = {}; off = 0
            for n, dims in specs:
                b = Buf(FA, off, dims); out[n] = b; off += b.n
            assert off <= 4352, off
            return out
        FAB = fcarve([("rstd", [256]), ("tmp", [2, 256]), ("rs2", [512]), ("t1", [512]), ("t2", [512]),
                      ("RD0", [512]), ("RD1", [512]), ("RB0", [256]), ("RB1", [256]), ("OU0", [256]), ("OU1", [256])])
        FC = fcarve([("rstd", [256]), ("tmp", [2, 256]), ("sg0", [512]), ("sg1", [512])])
        F0 = fcarve([("stage", [1024])])

        S.dma("sp", lambda e: e.dma_start(out=xT[:], in_=xT_in.rearrange("(k p) n -> p k n", p=128)), t_const,
              writes=[t for row in t_x for t in row])
        S.dma("sp", lambda e: e.dma_start(out=cT[:], in_=cxT_in.rearrange("(k p) n -> p k n", p=128)), t_c[0],
              writes=t_c)
        S.dma("pool", lambda e: e.dma_start(out=cosT[:], in_=cos_in), t_cos, writes=[t_cos])
        S.dma("pool", lambda e: e.dma_start(out=sinT[:], in_=sin_in), t_sin, writes=[t_sin])
        S.dma("sp", lambda e: e.dma_start(out=cmask[:], in_=cmask_in), t_cm, writes=[t_cm])
        S.dma("sp", lambda e: e.dma_start(out=cond[:], in_=cond_in), t_cond, writes=[t_cond])
        S.dma("pool", lambda e: e.dma_start(out=permb[:], in_=perm_in), t_perm, writes=[t_perm])
        epsD = sb("epsD", [128, 1], F32); eps64 = sb("eps64", [128, 1], F32)
        def init_const(e):
            e.memset(epsD[:], float(D * EPS))
            e.memset(eps64[:], float(64 * EPS))
            e.memset(ones_b[:], 1.0)
            e.memset(bones[:], 0.0)
            e.memset(bones[0:64, 0:64], 1.0)
            i = e.memset(bones[64:128, 64:128], 1.0)
            return i
        t_ones = T("ones")
        S.op("dve", init_const, writes=[t_ones])
        S.op("act", lambda e: e.activation(out=condb[:], in_=cond[:], func=AF.Silu), reads=[t_cond], writes=[t_condb])

        wq_bufs = ["Wq0", "Wq1", "Wq2", "Wq3"]
        t_named = {}
        def tn(name):
            if name not in t_named:
                t_named[name] = T(name)
            return t_named[name]
        rr = {"wq": 0, "wo": 0, "pt": 0, "ex": 0, "st": 0, "fin": 0, "acc": 0}

        def stream_tiles(s, blk):
            if s == 0:
                return [t_x[k][blk] for k in range(8)]
            return t_c

        def stream_ap(s, k, blk):
            if s == 0:
                return xT[:, k, blk * 256:(blk + 1) * 256]
            return cT[:, k, :]

        def stream_ap3(s, blk):
            if s == 0:
                return xT[:, :, blk * 256:(blk + 1) * 256]
            return cT[:, :, :]

        def norm_mod(s, blk, Gt, shj, hT, sqb, FX, ncols_off=0, heng="act", thT=None, ssr=None):
            st_t = stream_tiles(s, blk)
            if ssr is None:
                ssv = banks[0][:, 0:256]; tss = tbank[0]
            else:
                ssv, tss = ssr
            S.op("act", lambda e: e.activation(out=sqb[:, :, :], in_=stream_ap3(s, blk), func=AF.Square),
                 reads=st_t, writes=[tn("sqb")])
            def mm(e):
                for k in range(8):
                    i = e.matmul(ssv, lhsT=ones_b[:], rhs=sqb[:, k, :], start=(k == 0), stop=(k == 7))
                return i
            S.op("pe", mm, reads=[tn("sqb"), t_ones], writes=[tss])
            S.op("act", lambda e: e.activation(out=FX["rstd"][:, :], in_=ssv, func=AF.Ln, bias=epsD[:, 0:1], scale=1.0),
                 reads=[tss, t_ones], writes=[tn("rstd")])
            S.op("act", lambda e: e.activation(out=FX["rstd"][:, :], in_=FX["rstd"][:, :], func=AF.Exp, scale=-0.5),
                 reads=[tn("rstd")], writes=[tn("rstd")])
            for k in range(8):
                S.op("dve", lambda e, k=k: e.scalar_tensor_tensor(
                    out=FX["tmp"][:, k % 2, :], in0=stream_ap(s, k, blk), scalar=Gt[:, k, s:s + 1],
                    in1=FX["rstd"][:, :], op0=ALU.mult, op1=ALU.mult),
                    reads=[st_t[k], tn("rstd"), t_G], writes=[tn("tmp%d" % (k % 2))])
                S.op("dve", lambda e, k=k: e.tensor_scalar(
                    out=hT[:, k, ncols_off:ncols_off + 256], in0=FX["tmp"][:, k % 2, :],
                    scalar1=modT[:, shj + k, s:s + 1], scalar2=None, op0=ALU.add),
                    reads=[tn("tmp%d" % (k % 2)), t_mod], writes=[thT or tn("hT")])

        def qk_norm(ps, tps, gcol, out_ap, out_tiles, Wd, rope_blk=None, sq2=None, hat=None, R=None):
            if R is None:
                ssb = banks[0]; tss = tbank[0]; sfx = ""
                rs2 = FAB["rs2"]
            else:
                ssb = R["ssb"]; tss = R["tss"]; sfx = R["id"]; rs2 = R["rs2"]
            _tn = tn
            tn_ = lambda n: _tn(n + sfx)
            if Wd == 512:
                v3 = lambda ap: ap.rearrange("p (a b) -> p a b", a=2)
                bc = lambda ap: ap.rearrange("p (a b) -> p a b", a=1).broadcast_to([128, 2, 256])
            else:
                v3 = lambda ap: ap
                bc = lambda ap: ap
            t1 = FAB["t1"]; t2 = FAB["t2"]
            S.op("act", lambda e: e.activation(out=sq2[:, 0:Wd], in_=ps, func=AF.Square), reads=[tps], writes=[tn_("sq2")])
            S.op("pe", lambda e: e.matmul(ssb[:, 0:Wd], lhsT=bones[:], rhs=sq2[:, 0:Wd], start=True, stop=True),
                 reads=[tn_("sq2"), t_ones], writes=[tss])
            S.op("act", lambda e: e.activation(out=rs2[:, 0:Wd], in_=ssb[:, 0:Wd], func=AF.Ln, bias=eps64[:, 0:1], scale=1.0),
                 reads=[tss, t_ones], writes=[tn_("rs2")])
            S.op("act", lambda e: e.activation(out=rs2[:, 0:Wd], in_=rs2[:, 0:Wd], func=AF.Exp, scale=-0.5),
                 reads=[tn_("rs2")], writes=[tn_("rs2")])
            if rope_blk is None:
                S.op("dve", lambda e: e.scalar_tensor_tensor(out=out_ap, in0=v3(ps), scalar=gains[:, gcol:gcol + 1],
                                                             in1=v3(rs2[:, 0:Wd]), op0=ALU.mult, op1=ALU.mult),
                     reads=[tps, tn_("rs2"), t_gains], writes=out_tiles)
                return
            S.op("dve", lambda e: e.scalar_tensor_tensor(out=hat[:, 0:Wd], in0=ps, scalar=gains[:, gcol:gcol + 1],
                                                         in1=rs2[:, 0:Wd], op0=ALU.mult, op1=ALU.mult),
                 reads=[tps, tn_("rs2"), t_gains], writes=[tn_("hat")])
            S.op("pe", lambda e: e.matmul(ssb[:, 0:Wd], lhsT=permb[:], rhs=hat[:, 0:Wd], start=True, stop=True),
                 reads=[tn_("hat"), t_perm], writes=[tss])
            cs = slice(rope_blk * 256, (rope_blk + 1) * 256)
            S.op("dve", lambda e: e.tensor_tensor(out=v3(t1[:, 0:Wd]), in0=v3(ssb[:, 0:Wd]), in1=bc(sinT[:, cs]), op=ALU.mult),
                 reads=[tss, t_sin], writes=[tn_("t1")])
            S.op("dve", lambda e: e.tensor_tensor(out=v3(t2[:, 0:Wd]), in0=v3(hat[:, 0:Wd]), in1=bc(cosT[:, cs]), op=ALU.mult),
                 reads=[tn_("hat"), t_cos], writes=[tn_("t2")])
            S.op("dve", lambda e: e.tensor_tensor(out=out_ap, in0=v3(t1[:, 0:Wd]), in1=v3(t2[:, 0:Wd]), op=ALU.add),
                 reads=[tn_("t1"), tn_("t2")], writes=out_tiles)

        for li in range(NL):
            Wl = W[li]; last = last_flags[li]
            S.barrier()
            S.dma("sp", lambda e, Wl=Wl: e.dma_start(out=bada[:], in_=Wl["bada"]), t_bada, writes=[t_bada])
            S.dma("sp", lambda e, Wl=Wl: e.dma_start(out=norms[:], in_=Wl["norms"]), t_norms, writes=[t_norms])
            S.dma("sp", lambda e, Wl=Wl: e.dma_start(out=gains[:], in_=Wl["gains"]), t_gains, writes=[t_gains])
            mps = banks[1]; tmps = tbank[1]
            for n in range(12):
                wb = P0["wada%d" % (n % 2)]; twb = tn("wada%d" % (n % 2))
                S.dma("pool", lambda e, n=n, wb=wb, Wl=Wl: e.dma_start(
                    out=wb[:, :, :], in_=Wl["wada"][:, n * 512:(n + 1) * 512].rearrange("(k p) n -> p k n", p=128)),
                    twb, writes=[twb])
                def mm(e, n=n, wb=wb):
                    for jj in range(4):
                        j = n * 4 + jj
                        for k in range(8):
                            i = e.matmul(mps[:, 2 * j:2 * j + 2], lhsT=wb[:, k, jj * 128:(jj + 1) * 128],
                                         rhs=condb[:, 2 * k:2 * k + 2], start=(k == 0), stop=(k == 7))
                    return i
                S.op("pe", mm, reads=[twb, t_condb], writes=[tmps])
            for s in range(2):
                S.op("dve", lambda e, s=s: e.tensor_tensor(
                    out=modT[:, :, s], in0=mps[:, 0:96].rearrange("p (j s) -> p j s", s=2)[:, :, s], in1=bada[:, :],
                    op=ALU.add), reads=[tmps, t_bada], writes=[t_mod])
            for s in range(2):
                for (Gt, scj, ncol) in ((G1, 8, 0), (G2, 32, 8)):
                    S.op("dve", lambda e, s=s, Gt=Gt, scj=scj, ncol=ncol: e.scalar_tensor_tensor(
                        out=Gt[:, :, s], in0=modT[:, scj:scj + 8, s], scalar=1.0, in1=norms[:, ncol:ncol + 8],
                        op0=ALU.add, op1=ALU.mult), reads=[t_mod, t_norms], writes=[t_G])
                    S.op("dve", lambda e, s=s, Gt=Gt: e.tensor_scalar(
                        out=Gt[:, :, s], in0=Gt[:, :, s], scalar1=float(np.sqrt(D)), scalar2=None, op0=ALU.mult),
                        reads=[t_G], writes=[t_G])
            S.op("dve", lambda e: e.tensor_scalar(out=gains[:, 1:2], in0=gains[:, 1:2], scalar1=8.0, scalar2=None,
                                                  op0=ALU.mult), reads=[t_gains], writes=[t_gains])
            S.op("dve", lambda e: e.tensor_scalar(out=gains[:, 3:4], in0=gains[:, 3:4], scalar1=8.0, scalar2=None,
                                                  op0=ALU.mult), reads=[t_gains], writes=[t_gains])
            stg = F0["stage"]; tstg = tn("stage")
            for half in range(7):
                S.dma("sp", lambda e, half=half, Wl=Wl: e.dma_start(
                    out=stg[:, :], in_=Wl["rpbg"][:, half * 1024:(half + 1) * 1024]), tstg, writes=[tstg])
                S.op("act", lambda e: e.activation(out=stg[:, :], in_=stg[:, :], func=AF.Exp), reads=[tstg], writes=[tstg])
                S.op("dve", lambda e, half=half: e.tensor_tensor(
                    out=TBf[:, half * 2:(half + 1) * 2, :, :].rearrange("p f h c -> p (f h) c"),
                    in0=stg[:, :].rearrange("p (a c) -> p a c", c=64),
                    in1=cmask[:, :].rearrange("p (a c) -> p a c", a=1).broadcast_to([128, 16, 64]), op=ALU.mult),
                    reads=[tstg, t_cm], writes=[t_tb])
            def specials(e):
                e.memset(TBs4[:, 0, :, :], 0.0)
                e.tensor_copy(out=TBs4[:, 1, :, :], in_=TBf[:, 3, :, :])
                e.tensor_copy(out=TBm4[:, :, :, :], in_=TBf[:, 10:12, :, :])
                e.memset(TBs4[64:128, 1, :, :], 0.0)
                return e.memset(TBm4[0:64, 1, :, :], 0.0)
            S.op("pool", specials, reads=[t_tb], writes=[t_tb])

            if STOP < 1:
                break
            S.barrier()
            hT = PA["hT"]; sqb = PA["sqb"]
            S.op("dve", lambda e: e.memset(Va[:, :, :, 64:65], 1.0), writes=[t_va])
            S.op("dve", lambda e: e.memset(Vb[:, :, 64:65], 1.0), writes=[t_vb])
            for cc in range(4):
                S.dma("pool", lambda e, Wl=Wl, cc=cc: e.dma_start(
                    out=PA["Wk"][:, cc, :, :],
                    in_=Wl["win"][:, 512 + cc * 128:512 + (cc + 1) * 128].rearrange("(k p) n -> p k n", p=128)),
                    tn("Wk"), writes=[tn("Wk")])
            S.dma("pool", lambda e, Wl=Wl: e.dma_start(
                out=PA["Wk"][:, 4, :, :], in_=Wl["win"][:, 2048:2176].rearrange("(k p) n -> p k n", p=128)),
                tn("Wk"), writes=[tn("Wk")])
            S.dma("pool", lambda e, Wl=Wl: e.dma_start(
                out=PA["Wv"][:, :, 0:512], in_=Wl["win"][:, 1024:1536].rearrange("(k p) n -> p k n", p=128)),
                tn("Wv"), writes=[tn("Wv")])
            S.dma("pool", lambda e, Wl=Wl: e.dma_start(
                out=PA["Wv"][:, :, 512:640], in_=Wl["win"][:, 2176:2304].rearrange("(k p) n -> p k n", p=128)),
                tn("Wv"), writes=[tn("Wv")])
            hT2 = [PA["hT"], PA["hT1"]]
            RS = [dict(ssb=banks[3], tss=tbank[3], id="_a0", rs2=FAB["rs2"]),
                  dict(ssb=banks[4], tss=tbank[4], id="_a1", rs2=FAB["RD0"]),
                  dict(ssb=banks[7], tss=tbank[7], id="_a2", rs2=FAB["RD1"])]
            sq2s = [PA["sq2"], PA["sq2b"], PA["sq2c"]]

            def a_norm(tb):
                s = 0 if tb < 8 else 1
                blk = tb if tb < 8 else 0
                norm_mod(s, blk, G1, 0, hT2[tb % 2], sqb, FAB)

            a_norm(0)
            for tb in range(9):
                s = 0 if tb < 8 else 1
                blk = tb if tb < 8 else 0
                k0 = tb * 256
                hT = hT2[tb % 2]
                for pr in range(2):
                    pb = banks[1 + pr]; tpb = tbank[1 + pr]
                    def mm(e, pr=pr, pb=pb, hT=hT):
                        for j in range(2):
                            for k in range(8):
                                i = e.matmul(pb[:, j * 256:(j + 1) * 256], lhsT=PA["Wk"][:, 2 * pr + j, k, :], rhs=hT[:, k, :],
                                             start=(k == 0), stop=(k == 7))
                        return i
                    S.op("pe", mm, reads=[tn("Wk"), tn("hT")], writes=[tpb])
                def mmb(e, hT=hT):
                    for k in range(8):
                        i = e.matmul(banks[0][:, 256:512], lhsT=PA["Wk"][:, 4, k, :], rhs=hT[:, k, :], start=(k == 0), stop=(k == 7))
                    return i
                S.op("pe", mmb, reads=[tn("Wk"), tn("hT")], writes=[tbank[0]])
                for tt in range(2):
                    def mmv(e, tt=tt, hT=hT):
                        for k in range(8):
                            e.matmul(banks[5 + tt][:, 0:512], lhsT=hT[:, k, tt * 128:(tt + 1) * 128], rhs=PA["Wv"][:, k, 0:512],
                                     start=(k == 0), stop=(k == 7))
                        return e
                    def mmv2(e, tt=tt, hT=hT):
                        for k in range(8):
                            i = e.matmul(banks[7][:, 256 + tt * 128:256 + (tt + 1) * 128], lhsT=hT[:, k, tt * 128:(tt + 1) * 128],
                                         rhs=PA["Wv"][:, k, 512:640], start=(k == 0), stop=(k == 7))
                        return i
                    def mmva(e, tt=tt, hT=hT):
                        for k in range(8):
                            i = e.matmul(banks[5 + tt][:, 0:512], lhsT=hT[:, k, tt * 128:(tt + 1) * 128], rhs=PA["Wv"][:, k, 0:512],
                                         start=(k == 0), stop=(k == 7))
                        return i
                    S.op("pe", mmva, reads=[tn("Wv"), tn("hT")], writes=[tbank[5 + tt]])
                    S.op("pe", mmv2, reads=[tn("Wv"), tn("hT")], writes=[tbank[7]])
                streams = []
                for pr in range(2):
                    streams.append(S.capture(lambda pr=pr: qk_norm(
                        banks[1 + pr][:, 0:512], tbank[1 + pr], 1, KaT[:, 2 * pr:2 * pr + 2, k0:k0 + 256], [t_ka], 512,
                        sq2=sq2s[pr], R=RS[pr])))
                streams.append(S.capture(lambda: qk_norm(
                    banks[0][:, 256:512], tbank[0], 3, KbT[:, k0:k0 + 256], [t_kb], 256,
                    rope_blk=(blk if s == 0 else None), sq2=sq2s[2], hat=PA["khat"], R=RS[2])))
                def vev():
                    for tt in range(2):
                        kt = tb * 2 + tt
                        pv = banks[5 + tt][:, 0:512].rearrange("p (c e d) -> p c e d", c=4, e=2)
                        S.op("dve", lambda e, kt=kt, pv=pv: e.tensor_copy(out=Va[:, kt, :, 0:64], in_=pv[:, :, 0, :]),
                             reads=[tbank[5 + tt]], writes=[t_va])
                        S.op("dve", lambda e, kt=kt, pv=pv: e.tensor_copy(out=Va[:, kt, :, 65:129], in_=pv[:, :, 1, :]),
                             reads=[tbank[5 + tt]], writes=[t_va])
                        vb_ps = banks[7][:, 256 + tt * 128:256 + (tt + 1) * 128]
                        S.op("act", lambda e, kt=kt, vb_ps=vb_ps: e.activation(out=Vb[:, kt, 0:64], in_=vb_ps[:, 0:64], func=AF.Copy),
                             reads=[tbank[7]], writes=[t_vb])
                        S.op("act", lambda e, kt=kt, vb_ps=vb_ps: e.activation(out=Vb[:, kt, 65:129], in_=vb_ps[:, 64:128], func=AF.Copy),
                             reads=[tbank[7]], writes=[t_vb])
                streams.append(S.capture(vev))
                if tb + 1 < 9:
                    streams.append(S.capture(lambda: a_norm(tb + 1)))
                while any(streams):
                    for st_ in streams:
                        if st_:
                            S.play(st_.pop(0))

            if STOP < 2:
                break
            S.barrier()
            hT = PB["hT"]; sqb = PB["sqb"]
            cur = {"oT": PB["oT0"], "toT": tn("oT0")}
            delayed = []
            nqb = min(KNQB, 8 if last else 9)
            Qa2 = [PB["Qa0"], PB["Qa1"]]; Qb2 = [PB["Qb0"], PB["Qb1"]]

            def load_w(src_ap, pool="q", Wl=Wl):
                nm = wq_bufs[rr["wq"] % 4]; rr["wq"] += 1
                wbuf = PB[nm]
                S.dma("pool", lambda e, wbuf=wbuf, src_ap=src_ap: e.dma_start(
                    out=wbuf[:, :, :], in_=src_ap.rearrange("(k p) n -> p k n", p=128)), tn(nm), writes=[tn(nm)])
                return wbuf, tn(nm)

            def finalize(chunk, accb, taccb):
                i = rr["fin"] % 2; rr["fin"] += 1
                for d in [d for d in delayed if d[2] == i]:
                    delayed.remove(d); d[1]()
                RD = FAB["RD%d" % i]; RB = FAB["RB%d" % i]; OU = FAB["OU%d" % i]
                tRD = tn("RD%d" % i); tRB = tn("RB%d" % i); tOU = tn("OU%d" % i); tscr = tn("scr%d" % i)
                oT = cur["oT"]; toT = cur["toT"]
                S.op("act", lambda e: e.activation(out=RD[64:65, 0:256], in_=accb[64:65, 0:256], func=AF.Ln), reads=[taccb], writes=[tRD])
                S.op("act", lambda e: e.activation(out=RD[32:64, 256:512], in_=accb[32:64, 256:512], func=AF.Ln), reads=[taccb], writes=[tRD])
                S.op("act", lambda e: e.activation(out=RD[64:65, 0:256], in_=RD[64:65, 0:256], func=AF.Exp, scale=-1.0), reads=[tRD], writes=[tRD])
                S.op("act", lambda e: e.activation(out=RD[32:64, 256:512], in_=RD[32:64, 256:512], func=AF.Exp, scale=-1.0), reads=[tRD], writes=[tRD])
                S.op("dve", lambda e: e.tensor_copy(out=OU[0:64, :], in_=accb[0:64, 0:256]), reads=[taccb], writes=[tOU])
                S.op("dve", lambda e: e.tensor_copy(out=OU[64:128, :], in_=accb[64:128, 256:512]), reads=[taccb], writes=[tOU])
                S.dma("sp", lambda e: e.dma_start(out=scr[i, 1:2, :], in_=RD[64:65, 0:256]), tscr, reads=[tRD], writes=[tscr])
                S.dma("sp", lambda e: e.dma_start(out=scr[i, 0:1, :], in_=RD[63:64, 256:512]), tscr, reads=[tRD], writes=[tscr])
                S.dma("sp", lambda e: e.dma_start(out=RB[0:64, :], in_=scr[i, 1:2, :].partition_broadcast(64)), tRB,
                      reads=[tscr], writes=[tRB])
                S.dma("sp", lambda e: e.dma_start(out=RB[64:128, :], in_=scr[i, 0:1, :].partition_broadcast(64)), tRB,
                      reads=[tscr], writes=[tRB])
                def part2():
                    S.op("dve", lambda e: e.tensor_tensor(out=oT[:, chunk, :], in0=OU[:, :], in1=RB[:, :], op=ALU.mult),
                         reads=[tOU, tRB], writes=[toT])
                delayed.append([8, part2, i])

            def st_banks():
                bi = rr["st"] % 2; rr["st"] += 1
                ix, iy = (2, 4)[bi], (3, 7)[bi]
                return (banks[ix], tbank[ix], banks[iy], tbank[iy], PB["pt%d" % (2 * bi)], tn("pt%d" % (2 * bi)),
                        PB["pt%d" % (2 * bi + 1)], tn("pt%d" % (2 * bi + 1)))

            pend = []
            hops = []

            def play_hops(n):
                for _ in range(n):
                    if hops:
                        S.play(hops.pop(0))

            def emit_step(s1_pe, s1_rest, s2_pe, s2_post, nh=1):
                s1_pe()
                prev = pend.pop() if pend else None
                if prev:
                    prev[0]()
                s1_rest()
                if prev and prev[1]:
                    prev[1]()
                pend.append((s2_pe, s2_post))
                for d in list(delayed):
                    d[0] -= 1
                    if d[0] <= 0:
                        delayed.remove(d); d[1]()
                play_hops(cur.get("nh", nh))

            def flush_steps():
                if pend:
                    p = pend.pop()
                    p[0]()
                    if p[1]:
                        p[1]()
                for d in list(delayed):
                    delayed.remove(d); d[1]()

            def dense_attn(KT_ap, tK, Vfn, tV, Qbuf, c, tQ, ktiles, chunk):
                ai = 5 + rr["acc"] % 2; rr["acc"] += 1
                accb = banks[ai]; taccb = tbank[ai]
                npair = len(ktiles) // 2
                for n_i in range(npair):
                    kt0, kt1 = ktiles[2 * n_i], ktiles[2 * n_i + 1]
                    X, tX, Y, tY, pX, tpX, pY, tpY = st_banks()
                    def mmqk(e, kt0=kt0, kt1=kt1, X=X, Y=Y):
                        e.matmul(X[:, 0:256], lhsT=KT_ap(0, kt0), rhs=Qbuf[0:64, c, :], start=True, stop=True)
                        e.matmul(Y[:, 0:256], lhsT=KT_ap(1, kt0), rhs=Qbuf[64:128, c, :], start=True, stop=True)
                        e.matmul(X[:, 256:512], lhsT=KT_ap(0, kt1), rhs=Qbuf[0:64, c, :], start=True, stop=True)
                        return e.matmul(Y[:, 256:512], lhsT=KT_ap(1, kt1), rhs=Qbuf[64:128, c, :], start=True, stop=True)
                    first = n_i == 0; lastk = n_i == npair - 1
                    def mmpv(e, kt0=kt0, kt1=kt1, pX=pX, pY=pY, first=first, lastk=lastk):
                        e.matmul(accb[:, 0:256], lhsT=Vfn(kt0, 0), rhs=pX[:, 0:256], start=first, stop=False)
                        e.matmul(accb[:, 0:256], lhsT=Vfn(kt1, 0), rhs=pX[:, 256:512], start=False, stop=lastk)
                        e.matmul(accb[:, 256:512], lhsT=Vfn(kt0, 1), rhs=pY[:, 0:256], start=False, stop=False,
                                 skip_group_check=True)
                        return e.matmul(accb[:, 256:512], lhsT=Vfn(kt1, 1), rhs=pY[:, 256:512], start=False, stop=lastk,
                                        skip_group_check=True)
                    def s1_pe(mmqk=mmqk, tX=tX, tY=tY):
                        S.op("pe", mmqk, reads=[tK, tQ], writes=[tX, tY])
                    def s1_rest(X=X, Y=Y, pX=pX, pY=pY, tX=tX, tY=tY, tpX=tpX, tpY=tpY):
                        S.op("act", lambda e: e.activation(out=pX[:, :], in_=X[:, :], func=AF.Exp), reads=[tX], writes=[tpX])
                        S.op("act", lambda e: e.activation(out=pY[:, :], in_=Y[:, :], func=AF.Exp), reads=[tY], writes=[tpY])
                    def s2_pe(mmpv=mmpv, tpX=tpX, tpY=tpY):
                        S.op("pe", mmpv, reads=[tV, tpX, tpY], writes=[taccb])
                    s2_post = (lambda: finalize(chunk, accb, taccb)) if lastk else None
                    emit_step(s1_pe, s1_rest, s2_pe, s2_post)

            KaT_ap = lambda c: (lambda e_, kt: KaT[e_ * 64:(e_ + 1) * 64, c, kt * 128:(kt + 1) * 128])
            Va_fn = lambda c: (lambda kt, e_: (Va[:, kt, c, 0:128] if e_ == 0 else Va[:, kt, c, 1:129]))
            KbT_ap = lambda e_, kt: KbT[e_ * 64:(e_ + 1) * 64, kt * 128:(kt + 1) * 128]
            Vb_fn = lambda kt, e_: (Vb[:, kt, 0:128] if e_ == 0 else Vb[:, kt, 1:129])

            def na_unit(qb, hg, Qa_):
                tQa = tn("Qa%d" % (qb % 2))
                accs = [banks[5], banks[6]]; taccs = [tbank[5], tbank[6]]
                for sub in range(2):
                    r = qb * 4 + sub * 2
                    if r <= 2:
                        tiles = [(kr, "f") for kr in (0, 2, 4, 6)]
                    elif r >= 28:
                        tiles = [(kr, "f") for kr in (24, 26, 28, 30)]
                    else:
                        tiles = [(r + dl, "m") for dl in (-4, -2, 0, 2, 4)]
                    tiles = tiles + [(32, "c"), (34, "c")]
                    for n_i, (kr, kind) in enumerate(tiles):
                        X, tX, Y, tY, pX, tpX, pY, tpY = st_banks()
                        def mmqk(e, kr=kr, X=X, Y=Y, sub=sub):
                            for cl in range(2):
                                c = 2 * hg + cl
                                e.matmul(X[:, cl * 128:(cl + 1) * 128], lhsT=KaT[0:64, c, kr * 64:kr * 64 + 128],
                                         rhs=Qa_[0:64, c, sub * 128:(sub + 1) * 128], start=True, stop=True)
                                i = e.matmul(Y[:, cl * 128:(cl + 1) * 128], lhsT=KaT[64:128, c, kr * 64:kr * 64 + 128],
                                             rhs=Qa_[64:128, c, sub * 128:(sub + 1) * 128], start=True, stop=True)
                            return i
                        def s1_pe(mmqk=mmqk, tX=tX, tY=tY):
                            S.op("pe", mmqk, reads=[t_ka, tQa], writes=[tX, tY])
                        def s1_rest(kind=kind, kr=kr, r=r, X=X, Y=Y, pX=pX, pY=pY, tX=tX, tY=tY, tpX=tpX, tpY=tpY):
                            if kind == "c":
                                S.op("act", lambda e: e.activation(out=pX[:, 0:256], in_=X[:, 0:256], func=AF.Exp),
                                     reads=[tX], writes=[tpX])
                                S.op("act", lambda e: e.activation(out=pY[:, 0:256], in_=Y[:, 0:256], func=AF.Exp),
                                     reads=[tY], writes=[tpY])
                                return
                            dl = kr - r
                            if kind == "m" and dl == 4:
                                tbase, f0 = TBs4, 0
                            elif kind == "m" and dl == -4:
                                tbase, f0 = TBm4, 0
                            else:
                                tbase, f0 = TBf, 6 - dl
                            tv = tbase[:].rearrange("p f (c e) q -> p f c e q", e=2)
                            for e_, (B_, tB_, p_, tp_) in enumerate(((X, tX, pX, tpX), (Y, tY, pY, tpY))):
                                ex = PB["ex%d" % e_]; tex = tn("ex%d" % e_)
                                S.op("act", lambda e, B_=B_, ex=ex: e.activation(out=ex[:, 0:256], in_=B_[:, 0:256], func=AF.Exp),
                                     reads=[tB_], writes=[tex])
                                tab = tv[:, f0:f0 + 2, 2 * hg:2 * hg + 2, e_, :].rearrange("p f c q -> p c f q")
                                S.op("dve", lambda e, ex=ex, p_=p_, tab=tab: e.tensor_tensor(
                                    out=p_[:, 0:256].rearrange("p (c f q) -> p c f q", c=2, f=2),
                                    in0=ex[:, 0:256].rearrange("p (c f q) -> p c f q", c=2, f=2),
                                    in1=tab, op=ALU.mult), reads=[tex, t_tb], writes=[tp_])
                        first = n_i == 0; lastk = n_i == len(tiles) - 1
                        def mmpv(e, kr=kr, pX=pX, pY=pY, first=first, lastk=lastk, sub=sub):
                            for cl in range(2):
                                c = 2 * hg + cl
                                acc = accs[cl]
                                e.matmul(acc[:, sub * 128:(sub + 1) * 128], lhsT=Va[:, kr // 2, c, 0:128],
                                         rhs=pX[:, cl * 128:(cl + 1) * 128], start=first, stop=lastk)
                                i = e.matmul(acc[:, 256 + sub * 128:256 + (sub + 1) * 128], lhsT=Va[:, kr // 2, c, 1:129],
                                             rhs=pY[:, cl * 128:(cl + 1) * 128], start=False, stop=lastk,
                                             skip_group_check=True)
                            return i
                        def s2_pe(mmpv=mmpv, tpX=tpX, tpY=tpY):
                            S.op("pe", mmpv, reads=[t_va, tpX, tpY], writes=taccs)
                        s2_post = None
                        if lastk and sub == 1:
                            s2_post = lambda: [finalize(2 * hg + cl, accs[cl], taccs[cl]) for cl in range(2)]
                        emit_step(s1_pe, s1_rest, s2_pe, s2_post)

            def q_units(qb):
                s = 0 if qb < 8 else 1
                blk = qb if qb < 8 else 0
                par = qb % 2
                units = [lambda: norm_mod(s, blk, G1, 0, hT, sqb, FAB)]
                for pr in range(4):
                    def u(pr=pr):
                        isb = pr >= 2
                        col0 = pr * 256 if not isb else 1536 + (pr - 2) * 256
                        wbufs = [load_w(Wl["win"][:, col0 + j * 128:col0 + (j + 1) * 128]) for j in range(2)]
                        pb = banks[1]; tpb = tbank[1]
                        def mm(e):
                            for j in range(2):
                                for k in range(8):
                                    i = e.matmul(pb[:, j * 256:(j + 1) * 256], lhsT=wbufs[j][0][:, k, :], rhs=hT[:, k, :],
                                                 start=(k == 0), stop=(k == 7))
                            return i
                        S.op("pe", mm, reads=[wbufs[0][1], wbufs[1][1], tn("hT")], writes=[tpb])
                        if not isb:
                            qk_norm(pb[:, 0:512], tpb, 0, Qa2[par][:, 2 * pr:2 * pr + 2, :], [tn("Qa%d" % par)], 512,
                                    sq2=PB["sq2"])
                        else:
                            qk_norm(pb[:, 0:512], tpb, 2, Qb2[par][:, 2 * (pr - 2):2 * (pr - 2) + 2, :], [tn("Qb%d" % par)],
                                    512, rope_blk=(blk if s == 0 else None), sq2=PB["sq2"], hat=PB["qhat"])
                    units.append(u)
                return units

            def att_units(qb):
                par = qb % 2
                Qa_ = Qa2[par]; Qb_ = Qb2[par]; tQa = tn("Qa%d" % par); tQb = tn("Qb%d" % par)
                if qb == 8:
                    ua = [(lambda c=c: dense_attn(KaT_ap(c), t_ka, Va_fn(c), t_va, Qa_, c, tQa, [16, 17], c)) for c in range(4)]
                    ub = [(lambda c=c: dense_attn(KbT_ap, t_kb, Vb_fn, t_vb, Qb_, c, tQb, [16, 17], 4 + c)) for c in range(4)]
                    return ua + ub
                ug = [(lambda c=c: dense_attn(KbT_ap, t_kb, Vb_fn, t_vb, Qb_, c, tQb, list(range(18)), 4 + c)) for c in range(4)]
                un = [(lambda hg=hg: na_unit(qb, hg, Qa_)) for hg in range(2)]
                return [ug[0], un[0], ug[1], ug[2], un[1], ug[3]]

            def wout_unit(qb):
                s = 0 if qb < 8 else 1
                blk = qb if qb < 8 else 0
                oT = PB["oT%d" % (qb % 2)]; toT = tn("oT%d" % (qb % 2))
                st_t = stream_tiles(s, blk)
                for oc in range(8):
                    wbuf, twb = load_w(Wl["wout"][:, oc * 128:(oc + 1) * 128], pool="o")
                    pb = banks[1][:, (oc % 2) * 256:(oc % 2) * 256 + 256]; tpb = tbank[1]
                    def mm(e, wbuf=wbuf, pb=pb):
                        for k in range(8):
                            i = e.matmul(pb, lhsT=wbuf[:, k, :], rhs=oT[:, k, :], start=(k == 0), stop=(k == 7))
                        return i
                    S.op("pe", mm, reads=[twb, toT], writes=[tpb])
                    S.op("dve", lambda e, oc=oc, pb=pb: e.scalar_tensor_tensor(
                        out=stream_ap(s, oc, blk), in0=pb, scalar=modT[:, 16 + oc, s:s + 1],
                        in1=stream_ap(s, oc, blk), op0=ALU.mult, op1=ALU.add),
                        reads=[tpb, t_mod, st_t[oc]], writes=[st_t[oc]])

            for u in q_units(0):
                u()
            for qb in range(nqb):
                cur["oT"] = PB["oT%d" % (qb % 2)]; cur["toT"] = tn("oT%d" % (qb % 2))
                wl = S.capture(lambda: wout_unit(qb - 1)) if qb >= 1 else []
                qu = q_units(qb + 1) if qb + 1 < nqb else []
                qn = S.capture(qu[0]) if qu else []
                qp = S.capture(lambda: [u() for u in qu[1:]]) if qu else []
                while wl or qn:
                    if wl:
                        hops.append(wl.pop(0))
                    if qn:
                        hops.append(qn.pop(0))
                hops.extend(qp)
                nsteps = 64 if qb < 8 else 8
                cur["nh"] = -(-len(hops) // nsteps)
                for u in att_units(qb):
                    u()
                flush_steps()
                play_hops(len(hops))
            wout_unit(nqb - 1)

            if STOP < 3:
                break
            S.barrier()
            h2s = [PC["h2"], PC["h2b"]]; th2 = [tn("h2_0"), tn("h2_1")]
            sqb = PC["sqb"]; act = PC["act"]
            units = [(0, b) for b in range(8)] + ([] if last else [(1, 0)])
            fblocks = [units[i:i + 3] for i in range(0, len(units), 3)]
            def c_norm(ui):
                for j, (s, b) in enumerate(fblocks[ui]):
                    norm_mod(s, b, G2, 24, h2s[ui % 2], sqb, FC, ncols_off=j * 256, thT=th2[ui % 2],
                             ssr=(banks[1][:, 256:512], tbank[1]))
            c_norm(0)
            for ui, ublk in enumerate(fblocks):
                nb = len(ublk)
                halves = [(0, 512), (512, 256)] if nb == 3 else [(0, 512)]
                h2 = h2s[ui % 2]; th2_ = th2[ui % 2]
                nhops = S.capture(lambda: c_norm(ui + 1)) if ui + 1 < len(fblocks) else []
                per_f = -(-len(nhops) // 20)
                for f in range(22):
                    for _ in range(per_f):
                        if nhops:
                            S.play(nhops.pop(0))
                    wg = PC["Wg%d" % (f % 2)]; twg = tn("Wg%d" % (f % 2))
                    wu = PC["Wu%d" % (f % 2)]; twu = tn("Wu%d" % (f % 2))
                    S.dma("pool", lambda e, f=f, wg=wg, Wl=Wl: e.dma_start(
                        out=wg[:, :, :], in_=Wl["wg"][:, f * 128:(f + 1) * 128].rearrange("(k p) n -> p k n", p=128)),
                        twg, writes=[twg])
                    S.dma("pool", lambda e, f=f, wu=wu, Wl=Wl: e.dma_start(
                        out=wu[:, :, :], in_=Wl["wu"][:, f * 128:(f + 1) * 128].rearrange("(k p) n -> p k n", p=128)),
                        twu, writes=[twu])
                    gset = [banks[(f % 2) * 4 + hh] for hh in range(2)]; tg = [tbank[(f % 2) * 4 + hh] for hh in range(2)]
                    uset = [banks[(f % 2) * 4 + 2 + hh] for hh in range(2)]; tu = [tbank[(f % 2) * 4 + 2 + hh] for hh in range(2)]
                    def mmg(e, wg=wg, gset=gset, halves=halves, h2=h2):
                        for k in range(8):
                            for hh, (c0, w) in enumerate(halves):
                                i = e.matmul(gset[hh][:, 0:w], lhsT=wg[:, k, :], rhs=h2[:, k, c0:c0 + w],
                                             start=(k == 0), stop=(k == 7))
                        return i
                    S.op("pe", mmg, reads=[twg, th2_], writes=tg[:len(halves)])
                    def mmu(e, wu=wu, uset=uset, halves=halves, h2=h2):
                        for k in range(8):
                            for hh, (c0, w) in enumerate(halves):
                                i = e.matmul(uset[hh][:, 0:w], lhsT=wu[:, k, :], rhs=h2[:, k, c0:c0 + w],
                                             start=(k == 0), stop=(k == 7))
                        return i
                    S.op("pe", mmu, reads=[twu, th2_], writes=tu[:len(halves)])
                    for hh, (c0, w) in enumerate(halves):
                        sg = FC["sg%d" % hh]; tsg = tn("sg%d" % hh)
                        S.op("act", lambda e, sg=sg, g=gset[hh], w=w: e.activation(out=sg[:, 0:w], in_=g[:, 0:w], func=AF.Silu),
                             reads=[tg[hh]], writes=[tsg])
                        S.op("dve", lambda e, sg=sg, u=uset[hh], f=f, c0=c0, w=w: e.tensor_tensor(
                            out=act[:, f, c0:c0 + w], in0=u[:, 0:w], in1=sg[:, 0:w], op=ALU.mult),
                            reads=[tu[hh], tsg], writes=[tn("act")])
                while nhops:
                    S.play(nhops.pop(0))
                for oc in range(8):
                    wd = PC["Wd%d" % (oc % 2)]; twd = tn("Wd%d" % (oc % 2))
                    S.dma("pool", lambda e, oc=oc, wd=wd, Wl=Wl: e.dma_start(
                        out=wd[:, :, :], in_=Wl["wd"][:, oc * 128:(oc + 1) * 128].rearrange("(f p) n -> p f n", p=128)),
                        twd, writes=[twd])
                    yset = [banks[(oc % 2) * 2 + hh] for hh in range(2)]; ty = [tbank[(oc % 2) * 2 + hh] for hh in range(2)]
                    def mmd(e, wd=wd, yset=yset, halves=halves):
                        for f in range(22):
                            for hh, (c0, w) in enumerate(halves):
                                i = e.matmul(yset[hh][:, 0:w], lhsT=wd[:, f, :], rhs=act[:, f, c0:c0 + w],
                                             start=(f == 0), stop=(f == 21))
                        return i
                    S.op("pe", mmd, reads=[twd, tn("act")], writes=ty[:len(halves)])
                    for hh, (c0, w) in enumerate(halves):
                        for jj in range(w // 256):
                            s, blk = ublk[(c0 + jj * 256) // 256]
                            S.op("dve", lambda e, oc=oc, y=yset[hh], s=s, blk=blk, jj=jj: e.scalar_tensor_tensor(
                                out=stream_ap(s, oc, blk), in0=y[:, jj * 256:(jj + 1) * 256], scalar=modT[:, 40 + oc, s:s + 1],
                                in1=stream_ap(s, oc, blk), op0=ALU.mult, op1=ALU.add),
                                reads=[ty[hh], t_mod, stream_tiles(s, blk)[oc]], writes=[stream_tiles(s, blk)[oc]])

        S.barrier()
        S.dma("sp", lambda e: e.dma_start(out=xT_out.rearrange("(k p) n -> p k n", p=128), in_=xT[:]), t_out,
              reads=[t for row in t_x for t in row])
        S.dma("sp", lambda e: e.dma_start(out=cxT_out.rearrange("(k p) n -> p k n", p=128), in_=cT[:]), t_out,
              reads=t_c)
        S.final.append(t_out)
        S.emit()
    return nc


def _const_tables():
    t = np.arange(NT)
    half = 32
    inv = (10000.0 ** (-np.arange(0, half, 2, dtype=np.float32) / half)).astype(np.float32)
    row = (t // 64).astype(np.float32)[:, None] * inv
    col = (t % 64).astype(np.float32)[:, None] * inv
    cr, sr, cc, sc = np.cos(row), np.sin(row), np.cos(col), np.sin(col)
    cos64 = np.concatenate([cr, cr, cc, cc], axis=1).T
    sin64 = np.concatenate([-sr, sr, -sc, sc], axis=1).T
    cosT = np.concatenate([cos64, cos64], axis=0).astype(np.float32)
    sinT = np.concatenate([sin64, sin64], axis=0).astype(np.float32)
    pi = np.zeros(64, dtype=np.int64)
    for i in range(64):
        base = (i // 32) * 32; j = i % 32
        pi[i] = base + (j + 16) % 32
    perm = np.zeros((128, 128), dtype=np.float32)
    for hh in range(2):
        for i in range(64):
            perm[hh * 64 + pi[i], hh * 64 + i] = 1.0
    cidx = np.arange(64)
    cs = np.clip(cidx - 8, 0, 48)
    col_ok = (cidx[None, :] >= cs[:, None]) & (cidx[None, :] < cs[:, None] + 16)
    cm = col_ok.T.astype(np.float32)
    cmask = np.concatenate([cm, cm], axis=0)
    return np.ascontiguousarray(cosT), np.ascontiguousarray(sinT), perm, np.ascontiguousarray(cmask)


def _layer_arrays(l, w_ada, b_ada, attn_norm, w_in, q_norm_a, k_norm_a, q_norm_b, k_norm_b, rpb, w_out, ffn_norm,
                  w_gate, w_up, w_down):
    fm = lambda v: np.ascontiguousarray(v.reshape(-1, 128).T)
    hb = np.array([c + 4 * e for c in range(4) for e in range(2)])
    colperm = np.arange(2304)
    colperm[1536:2048] = (1536 + hb[:, None] * 64 + np.arange(64)[None, :]).reshape(-1)
    rowperm = np.arange(1024)
    rowperm[512:1024] = (512 + hb[:, None] * 64 + np.arange(64)[None, :]).reshape(-1)
    cidx = np.arange(64)
    dc = np.clip(cidx[:, None] - cidx[None, :], -15, 15) + 15
    g = np.empty((2, 64, NFI, 8, 64), dtype=np.float32)
    for j in range(2):
        for fi in range(NFI):
            dr = 13 + j - fi
            g[j, :, fi, :, :] = np.transpose(rpb[l][:, dr, :][:, dc], (1, 0, 2))
    gains = np.stack([np.tile(q_norm_a[l], 2), np.tile(k_norm_a[l], 2), np.tile(q_norm_b[l], 2), np.tile(k_norm_b[l], 2)], axis=1)
    return dict(
        wada=w_ada[l], bada=fm(b_ada[l]),
        norms=np.ascontiguousarray(np.concatenate([fm(attn_norm[l]), fm(ffn_norm[l])], axis=1)),
        gains=np.ascontiguousarray(gains.astype(np.float32)),
        win=np.ascontiguousarray(w_in[l][:, colperm]), wout=np.ascontiguousarray(w_out[l][rowperm, :]),
        wg=w_gate[l], wu=w_up[l], wd=w_down[l], rpbg=np.ascontiguousarray(g.reshape(128, NFI * 512)))


_PROGS = {}


def _get_prog(nl, last_flags):
    key = (nl, tuple(last_flags))
    if key not in _PROGS:
        _PROGS[key] = build_program(list(range(nl)), list(last_flags))
    return _PROGS[key]


FUSED_LAYERS = 4


def kernel(x, c, ctx, c_ctx, w_ada, b_ada, attn_norm, w_in, q_norm_a, k_norm_a, q_norm_b, k_norm_b, rpb, w_out,
           ffn_norm, w_gate, w_up, w_down):
    f32 = lambda a: np.asarray(a, dtype=np.float32)
    x, c, ctx, c_ctx = f32(x), f32(c), f32(ctx), f32(c_ctx)
    ws = [f32(a) for a in (w_ada, b_ada, attn_norm, w_in, q_norm_a, k_norm_a, q_norm_b, k_norm_b, rpb, w_out,
                           ffn_norm, w_gate, w_up, w_down)]
    B = x.shape[0]
    cosT, sinT, perm, cmask = _const_tables()
    LA = [_layer_arrays(l, *ws) for l in range(DEPTH)]
    xT = [np.ascontiguousarray(x[b].T) for b in range(B)]
    cxT = [np.ascontiguousarray(ctx[b].T) for b in range(B)]
    conds = []
    for b in range(B):
        cd = np.empty((128, 8, 2), dtype=np.float32)
        cd[:, :, 0] = c[b].reshape(8, 128).T
        cd[:, :, 1] = c_ctx.reshape(8, 128).T
        conds.append(np.ascontiguousarray(cd.reshape(128, 16)))
    l = 0
    while l < DEPTH:
        nl = min(FUSED_LAYERS, DEPTH - l)
        flags = [(l + i) == DEPTH - 1 for i in range(nl)]
        nc = _get_prog(nl, flags)
        in_maps = []
        for b in range(B):
            m = {"xT_in": xT[b], "cxT_in": cxT[b], "cond": conds[b], "cosT": cosT, "sinT": sinT, "cmask": cmask,
                 "perm": perm}
            for i in range(nl):
                for k, v in LA[l + i].items():
                    m["%s%d" % (k, i)] = v
            in_maps.append(m)
        res = run_bass_kernel_spmd(nc, in_maps, core_ids=list(range(B)))
        xT = [np.asarray(res.results[b]["xT_out"]) for b in range(B)]
        cxT = [np.asarray(res.results[b]["cxT_out"]) for b in range(B)]
        l += nl
    out = np.stack([xT[b].T for b in range(B)], axis=0).astype(np.float32)
    return out
```

```python
import os
import numpy as np
import concourse.bass as bass
import concourse.mybir as mybir
from concourse.bass_utils import run_bass_kernel_spmd
from contextlib import ExitStack

F32 = mybir.dt.float32
BF16 = mybir.dt.bfloat16
AF = mybir.ActivationFunctionType
ALU = mybir.AluOpType

D = 1024; NT = 2048; NCX = 256; DEPTH = 4; DFF = 2816; NKEY = NT + NCX
EPS = 1e-6
EPOCH = 12000
STOP = int(os.environ.get('KSTOP', '9'))
KSUB = int(os.environ.get('KSUB', '9'))
KNQB = int(os.environ.get('KNQB', '99'))
KD = int(os.environ.get('KD', '9'))
NFI = 14


class T:
    __slots__ = ("name", "w", "rd", "sem", "cnt", "excl")

    def __init__(self, name, excl=False):
        self.name = name; self.w = None; self.rd = []; self.sem = None; self.cnt = 0; self.excl = excl


class Op:
    __slots__ = ("eng", "fn", "deps", "sig", "signal", "is_dma", "sem_tile")


class Sched:
    def __init__(self, nc, stack):
        self.nc = nc; self.stack = stack
        self.ops = {e: [] for e in ("pe", "act", "dve", "pool", "sp")}
        self.final = []
        self.pending = {}
        self.defer = None

    def _deps(self, op, reads, writes):
        ex = [t for t in reads if t.excl]
        if ex:
            writes = list(writes) + [t for t in ex if t not in writes]
        deps = {}
        for t in reads:
            if t.w is not None:
                deps[id(t.w)] = (t.w, True)
        for t in writes:
            if t.w is not None and id(t.w) not in deps:
                deps[id(t.w)] = (t.w, False)
            for r in t.rd:
                if id(r) not in deps:
                    deps[id(r)] = (r, False)
        for p in self.pending.pop(op.eng, ()):
            if id(p) not in deps:
                deps[id(p)] = (p, True)
        keep = []
        for p, raw in deps.values():
            if p is op:
                continue
            if p.eng == op.eng and not p.is_dma and not op.is_dma and not raw:
                continue
            p.sig = True
            keep.append(p)
        op.deps = keep
        for t in reads:
            t.rd.append(op)
        for t in writes:
            t.w = op; t.rd = []

    def capture(self, fn):
        self.defer = []
        fn()
        items = self.defer
        self.defer = None
        return items

    def play(self, item):
        kind, args = item
        if kind == "op":
            self.op(*args)
        else:
            self.dma(*args)

    def op(self, eng, fn, reads=(), writes=()):
        if self.defer is not None:
            self.defer.append(("op", (eng, fn, list(reads), list(writes))))
            return None
        o = Op()
        o.eng = eng; o.fn = fn; o.sig = False; o.signal = None; o.is_dma = False; o.sem_tile = None
        self._deps(o, reads, writes)
        self.ops[eng].append(o)
        return o

    def dma(self, eng, fn, sem_tile, reads=(), writes=()):
        if self.defer is not None:
            self.defer.append(("dma", (eng, fn, sem_tile, list(reads), list(writes))))
            return None
        o = Op()
        o.eng = eng; o.fn = fn; o.sig = True; o.is_dma = True
        if sem_tile.sem is None:
            sem_tile.sem = self.stack.enter_context(self.nc.semaphore("d_" + sem_tile.name))
        sem_tile.cnt += 1
        o.sem_tile = sem_tile
        o.signal = (sem_tile.sem, 16 * sem_tile.cnt, None)
        self._deps(o, reads, writes)
        self.ops[eng].append(o)
        return o

    def barrier(self):
        lasts = [lst[-1] for lst in self.ops.values() if lst]
        for e in self.ops:
            self.pending[e] = list(lasts)

    def emit(self):
        nc = self.nc
        esems = {}
        for e, lst in self.ops.items():
            k = 0
            for o in lst:
                if o.is_dma or not o.sig:
                    continue
                ep = k // EPOCH
                if (e, ep) not in esems:
                    esems[(e, ep)] = self.stack.enter_context(nc.semaphore("e_%s_%d" % (e, ep)))
                o.signal = (esems[(e, ep)], k % EPOCH + 1, (e, k))
                k += 1
        handles = {"pe": "tensor", "act": "scalar", "dve": "vector", "pool": "gpsimd", "sp": "sync"}
        final = self.final

        def make(e, lst):
            def body(eng):
                waited_c = {}
                waited_d = {}
                for o in lst:
                    for p in o.deps:
                        sem, val, key = p.signal
                        if key is not None:
                            if waited_c.get(key[0], -1) >= key[1]:
                                continue
                            waited_c[key[0]] = key[1]
                        else:
                            if waited_d.get(id(sem), 0) >= val:
                                continue
                            waited_d[id(sem)] = val
                        eng.wait_ge(sem, val)
                    ins = o.fn(eng)
                    if o.sig:
                        ins.then_inc(o.signal[0], 16 if o.is_dma else 1)
                if e == "sp":
                    for t in final:
                        if t.sem is not None:
                            eng.wait_ge(t.sem, 16 * t.cnt)
            return body

        with nc.Block() as block:
            for e, lst in self.ops.items():
                if lst or (e == "sp" and final):
                    getattr(block, handles[e])(make(e, lst))


class Buf:
    def __init__(self, base, off, dims):
        self.base = base; self.off = off; self.dims = list(dims)
        n = 1
        for d in dims:
            n *= d
        self.n = n
        v = base[:, off:off + n]
        if len(dims) == 2:
            v = v.rearrange("p (a b) -> p a b", a=dims[0], b=dims[1])
        elif len(dims) == 3:
            v = v.rearrange("p (a b c) -> p a b c", a=dims[0], b=dims[1], c=dims[2])
        self.v = v

    def __getitem__(self, idx):
        return self.v[idx]


def build_program(layers, last_flags, load_name=("xT_in", "cxT_in")):
    nc = bass.Bass("TRN2", target_bir_lowering=False)
    NL = len(layers)
    dram_in = lambda n, s: nc.dram_tensor(n, s, F32, kind="ExternalInput").ap()
    xT_in = dram_in("xT_in", [D, NT]); cxT_in = dram_in("cxT_in", [D, NCX])
    cond_in = dram_in("cond", [128, 16])
    cos_in = dram_in("cosT", [128, NT]); sin_in = dram_in("sinT", [128, NT])
    cmask_in = dram_in("cmask", [128, 64]); perm_in = dram_in("perm", [128, 128])
    W = []
    for i in range(NL):
        W.append(dict(
            wada=dram_in("wada%d" % i, [D, 6 * D]), bada=dram_in("bada%d" % i, [128, 48]),
            norms=dram_in("norms%d" % i, [128, 16]), gains=dram_in("gains%d" % i, [128, 4]),
            win=dram_in("win%d" % i, [D, 2304]), wout=dram_in("wout%d" % i, [D, D]),
            wg=dram_in("wg%d" % i, [D, DFF]), wu=dram_in("wu%d" % i, [D, DFF]),
            wd=dram_in("wd%d" % i, [DFF, D]), rpbg=dram_in("rpbg%d" % i, [128, NFI * 512])))
    xT_out = nc.dram_tensor("xT_out", [D, NT], F32, kind="ExternalOutput").ap()
    cxT_out = nc.dram_tensor("cxT_out", [D, NCX], F32, kind="ExternalOutput").ap()
    scr = nc.dram_tensor("scr", [8, 2, 256], F32).ap()

    with ExitStack() as st:
        S = Sched(nc, st)
        sb = lambda n, s, d: st.enter_context(nc.sbuf_tensor(n, s, d))
        xT = sb("xT", [128, 8, NT], F32); cT = sb("cT", [128, 8, NCX], F32)
        cosT = sb("cosT_s", [128, NT], BF16); sinT = sb("sinT_s", [128, NT], BF16)
        TBf = sb("TBf", [128, NFI, 8, 64], BF16)
        TBs4 = sb("TBs4", [128, 2, 8, 64], BF16); TBm4 = sb("TBm4", [128, 2, 8, 64], BF16)
        cmask = sb("cmask_s", [128, 64], F32)
        ones_b = sb("ones_b", [128, 128], BF16); bones = sb("bones", [128, 128], BF16)
        permb = sb("permb", [128, 128], BF16)
        cond = sb("cond_s", [128, 16], F32); condb = sb("condb", [128, 16], BF16)
        modT = sb("modT", [128, 48, 2], F32); bada = sb("bada_s", [128, 48], F32)
        norms = sb("norms_s", [128, 16], F32); gains = sb("gains_s", [128, 4], F32)
        G1 = sb("G1", [128, 8, 2], F32); G2 = sb("G2", [128, 8, 2], F32)
        BA = sb("BA", [128, 43700], BF16)
        FA = sb("FA", [128, 4352], F32)
        banks = [st.enter_context(nc.psum_tensor("bank%d" % i, [128, 512], F32)) for i in range(8)]
        tbank = [T("bank%d" % i, excl=True) for i in range(8)]

        t_x = [[T("x%d_%d" % (k, b)) for b in range(8)] for k in range(8)]
        t_c = [T("c%d" % k) for k in range(8)]
        t_cos = T("cos"); t_sin = T("sin"); t_tb = T("tb"); t_cm = T("cmask"); t_const = T("const")
        t_perm = T("perm"); t_cond = T("cond"); t_condb = T("condb"); t_mod = T("modT")
        t_bada = T("bada"); t_norms = T("norms"); t_gains = T("gains"); t_G = T("G")
        t_out = T("out")

        o = 0
        KaT = Buf(BA, o, [4, NKEY]); o += 4 * NKEY
        KbT = Buf(BA, o, [NKEY]); o += NKEY
        Va = Buf(BA, o, [18, 4, 129]); o += 18 * 4 * 129
        Vb = Buf(BA, o, [18, 129]); o += 18 * 129
        KV_END = o
        t_ka = T("KaT"); t_kb = T("KbT"); t_va = T("Va"); t_vb = T("Vb")
        def carve(off, specs):
            out = {}
            for n, dims in specs:
                b = Buf(BA, off, dims); out[n] = b; off += b.n
            return out, off
        PA, endA = carve(KV_END, [("hT", [8, 256]), ("hT1", [8, 256]), ("sq2b", [512]), ("sq2c", [512]), ("sqb", [8, 256]), ("Wk", [5, 8, 128]), ("Wv", [8, 640]),
                                  ("sq2", [512]), ("khat", [512])])
        PB, endB = carve(KV_END, [("hT", [8, 256]), ("sqb", [8, 256]), ("Wq0", [8, 128]), ("Wq1", [8, 128]),
                                  ("Wq2", [8, 128]), ("Wq3", [8, 128]), ("sq2", [512]), ("qhat", [512]),
                                  ("Qa0", [4, 256]), ("Qa1", [4, 256]), ("Qb0", [4, 256]), ("Qb1", [4, 256]), ("pt0", [512]), ("pt1", [512]),
                                  ("pt2", [512]), ("pt3", [512]), ("ex0", [512]), ("ex1", [512]), ("oT0", [8, 256]), ("oT1", [8, 256])])
        PC, endC = carve(0, [("h2", [8, 768]), ("h2b", [8, 768]), ("sqb", [8, 256]), ("act", [22, 768]), ("Wg0", [8, 128]),
                             ("Wg1", [8, 128]), ("Wu0", [8, 128]), ("Wu1", [8, 128]), ("Wd0", [22, 128]),
                             ("Wd1", [22, 128])])
        P0, end0 = carve(0, [("wada0", [8, 512]), ("wada1", [8, 512])])
        assert max(endA, endB, endC, end0) <= 43700, (endA, endB, endC, end0)
        def fcarve(specs):
            out = {}; off = 0
            for n, dims in specs:
                b = Buf(FA, off, dims); out[n] = b; off += b.n
            assert off <= 4352, off
            return out
        FAB = fcarve([("rstd", [256]), ("tmp", [2, 256]), ("rs2", [512]), ("t1", [512]), ("t2", [512]),
                      ("RD0", [512]), ("RD1", [512]), ("RB0", [256]), ("RB1", [256]), ("OU0", [256]), ("OU1", [256])])
        FC = fcarve([("rstd", [256]), ("tmp", [2, 256]), ("sg0", [512]), ("sg1", [512])])
        F0 = fcarve([("stage", [1024])])

        S.dma("sp", lambda e: e.dma_start(out=xT[:], in_=xT_in.rearrange("(k p) n -> p k n", p=128)), t_const,
              writes=[t for row in t_x for t in row])
        S.dma("sp", lambda e: e.dma_start(out=cT[:], in_=cxT_in.rearrange("(k p) n -> p k n", p=128)), t_c[0],
              writes=t_c)
        S.dma("pool", lambda e: e.dma_start(out=cosT[:], in_=cos_in), t_cos, writes=[t_cos])
        S.dma("pool", lambda e: e.dma_start(out=sinT[:], in_=sin_in), t_sin, writes=[t_sin])
        S.dma("sp", lambda e: e.dma_start(out=cmask[:], in_=cmask_in), t_cm, writes=[t_cm])
        S.dma("sp", lambda e: e.dma_start(out=cond[:], in_=cond_in), t_cond, writes=[t_cond])
        S.dma("pool", lambda e: e.dma_start(out=permb[:], in_=perm_in), t_perm, writes=[t_perm])
        epsD = sb("epsD", [128, 1], F32); eps64 = sb("eps64", [128, 1], F32)
        def init_const(e):
            e.memset(epsD[:], float(D * EPS))
            e.memset(eps64[:], float(64 * EPS))
            e.memset(ones_b[:], 1.0)
            e.memset(bones[:], 0.0)
            e.memset(bones[0:64, 0:64], 1.0)
            i = e.memset(bones[64:128, 64:128], 1.0)
            return i
        t_ones = T("ones")
        S.op("dve", init_const, writes=[t_ones])
        S.op("act", lambda e: e.activation(out=condb[:], in_=cond[:], func=AF.Silu), reads=[t_cond], writes=[t_condb])

        wq_bufs = ["Wq0", "Wq1", "Wq2", "Wq3"]
        t_named = {}
        def tn(name):
            if name not in t_named:
                t_named[name] = T(name)
            return t_named[name]
        rr = {"wq": 0, "wo": 0, "pt": 0, "ex": 0, "st": 0, "fin": 0, "acc": 0}

        def stream_tiles(s, blk):
            if s == 0:
                return [t_x[k][blk] for k in range(8)]
            return t_c

        def stream_ap(s, k, blk):
            if s == 0:
                return xT[:, k, blk * 256:(blk + 1) * 256]
            return cT[:, k, :]

        def stream_ap3(s, blk):
            if s == 0:
                return xT[:, :, blk * 256:(blk + 1) * 256]
            return cT[:, :, :]

        def norm_mod(s, blk, Gt, shj, hT, sqb, FX, ncols_off=0, heng="act", thT=None, ssr=None):
            st_t = stream_tiles(s, blk)
            if ssr is None:
                ssv = banks[0][:, 0:256]; tss = tbank[0]
            else:
                ssv, tss = ssr
            S.op("act", lambda e: e.activation(out=sqb[:, :, :], in_=stream_ap3(s, blk), func=AF.Square),
                 reads=st_t, writes=[tn("sqb")])
            def mm(e):
                for k in range(8):
                    i = e.matmul(ssv, lhsT=ones_b[:], rhs=sqb[:, k, :], start=(k == 0), stop=(k == 7))
                return i
            S.op("pe", mm, reads=[tn("sqb"), t_ones], writes=[tss])
            S.op("act", lambda e: e.activation(out=FX["rstd"][:, :], in_=ssv, func=AF.Ln, bias=epsD[:, 0:1], scale=1.0),
                 reads=[tss, t_ones], writes=[tn("rstd")])
            S.op("act", lambda e: e.activation(out=FX["rstd"][:, :], in_=FX["rstd"][:, :], func=AF.Exp, scale=-0.5),
                 reads=[tn("rstd")], writes=[tn("rstd")])
            for k in range(8):
                S.op("dve", lambda e, k=k: e.scalar_tensor_tensor(
                    out=FX["tmp"][:, k % 2, :], in0=stream_ap(s, k, blk), scalar=Gt[:, k, s:s + 1],
                    in1=FX["rstd"][:, :], op0=ALU.mult, op1=ALU.mult),
                    reads=[st_t[k], tn("rstd"), t_G], writes=[tn("tmp%d" % (k % 2))])
                S.op("dve", lambda e, k=k: e.tensor_scalar(
                    out=hT[:, k, ncols_off:ncols_off + 256], in0=FX["tmp"][:, k % 2, :],
                    scalar1=modT[:, shj + k, s:s + 1], scalar2=None, op0=ALU.add),
                    reads=[tn("tmp%d" % (k % 2)), t_mod], writes=[thT or tn("hT")])

        def qk_norm(ps, tps, gcol, out_ap, out_tiles, Wd, rope_blk=None, sq2=None, hat=None, R=None):
            if R is None:
                ssb = banks[0]; tss = tbank[0]; sfx = ""
                rs2 = FAB["rs2"]
            else:
                ssb = R["ssb"]; tss = R["tss"]; sfx = R["id"]; rs2 = R["rs2"]
            _tn = tn
            tn_ = lambda n: _tn(n + sfx)
            if Wd == 512:
                v3 = lambda ap: ap.rearrange("p (a b) -> p a b", a=2)
                bc = lambda ap: ap.rearrange("p (a b) -> p a b", a=1).broadcast_to([128, 2, 256])
            else:
                v3 = lambda ap: ap
                bc = lambda ap: ap
            t1 = FAB["t1"]; t2 = FAB["t2"]
            S.op("act", lambda e: e.activation(out=sq2[:, 0:Wd], in_=ps, func=AF.Square), reads=[tps], writes=[tn_("sq2")])
            S.op("pe", lambda e: e.matmul(ssb[:, 0:Wd], lhsT=bones[:], rhs=sq2[:, 0:Wd], start=True, stop=True),
                 reads=[tn_("sq2"), t_ones], writes=[tss])
            S.op("act", lambda e: e.activation(out=rs2[:, 0:Wd], in_=ssb[:, 0:Wd], func=AF.Ln, bias=eps64[:, 0:1], scale=1.0),
                 reads=[tss, t_ones], writes=[tn_("rs2")])
            S.op("act", lambda e: e.activation(out=rs2[:, 0:Wd], in_=rs2[:, 0:Wd], func=AF.Exp, scale=-0.5),
                 reads=[tn_("rs2")], writes=[tn_("rs2")])
            if rope_blk is None:
                S.op("dve", lambda e: e.scalar_tensor_tensor(out=out_ap, in0=v3(ps), scalar=gains[:, gcol:gcol + 1],
                                                             in1=v3(rs2[:, 0:Wd]), op0=ALU.mult, op1=ALU.mult),
                     reads=[tps, tn_("rs2"), t_gains], writes=out_tiles)
                return
            S.op("dve", lambda e: e.scalar_tensor_tensor(out=hat[:, 0:Wd], in0=ps, scalar=gains[:, gcol:gcol + 1],
                                                         in1=rs2[:, 0:Wd], op0=ALU.mult, op1=ALU.mult),
                 reads=[tps, tn_("rs2"), t_gains], writes=[tn_("hat")])
            S.op("pe", lambda e: e.matmul(ssb[:, 0:Wd], lhsT=permb[:], rhs=hat[:, 0:Wd], start=True, stop=True),
                 reads=[tn_("hat"), t_perm], writes=[tss])
            cs = slice(rope_blk * 256, (rope_blk + 1) * 256)
            S.op("dve", lambda e: e.tensor_tensor(out=v3(t1[:, 0:Wd]), in0=v3(ssb[:, 0:Wd]), in1=bc(sinT[:, cs]), op=ALU.mult),
                 reads=[tss, t_sin], writes=[tn_("t1")])
            S.op("dve", lambda e: e.tensor_tensor(out=v3(t2[:, 0:Wd]), in0=v3(hat[:, 0:Wd]), in1=bc(cosT[:, cs]), op=ALU.mult),
                 reads=[tn_("hat"), t_cos], writes=[tn_("t2")])
            S.op("dve", lambda e: e.tensor_tensor(out=out_ap, in0=v3(t1[:, 0:Wd]), in1=v3(t2[:, 0:Wd]), op=ALU.add),
                 reads=[tn_("t1"), tn_("t2")], writes=out_tiles)

        for li in range(NL):
            Wl = W[li]; last = last_flags[li]
            S.barrier()
            S.dma("sp", lambda e, Wl=Wl: e.dma_start(out=bada[:], in_=Wl["bada"]), t_bada, writes=[t_bada])
            S.dma("sp", lambda e, Wl=Wl: e.dma_start(out=norms[:], in_=Wl["norms"]), t_norms, writes=[t_norms])
            S.dma("sp", lambda e, Wl=Wl: e.dma_start(out=gains[:], in_=Wl["gains"]), t_gains, writes=[t_gains])
            mps = banks[1]; tmps = tbank[1]
            for n in range(12):
                wb = P0["wada%d" % (n % 2)]; twb = tn("wada%d" % (n % 2))
                S.dma("pool", lambda e, n=n, wb=wb, Wl=Wl: e.dma_start(
                    out=wb[:, :, :], in_=Wl["wada"][:, n * 512:(n + 1) * 512].rearrange("(k p) n -> p k n", p=128)),
                    twb, writes=[twb])
                def mm(e, n=n, wb=wb):
                    for jj in range(4):
                        j = n * 4 + jj
                        for k in range(8):
                            i = e.matmul(mps[:, 2 * j:2 * j + 2], lhsT=wb[:, k, jj * 128:(jj + 1) * 128],
                                         rhs=condb[:, 2 * k:2 * k + 2], start=(k == 0), stop=(k == 7))
                    return i
                S.op("pe", mm, reads=[twb, t_condb], writes=[tmps])
            for s in range(2):
                S.op("dve", lambda e, s=s: e.tensor_tensor(
                    out=modT[:, :, s], in0=mps[:, 0:96].rearrange("p (j s) -> p j s", s=2)[:, :, s], in1=bada[:, :],
                    op=ALU.add), reads=[tmps, t_bada], writes=[t_mod])
            for s in range(2):
                for (Gt, scj, ncol) in ((G1, 8, 0), (G2, 32, 8)):
                    S.op("dve", lambda e, s=s, Gt=Gt, scj=scj, ncol=ncol: e.scalar_tensor_tensor(
                        out=Gt[:, :, s], in0=modT[:, scj:scj + 8, s], scalar=1.0, in1=norms[:, ncol:ncol + 8],
                        op0=ALU.add, op1=ALU.mult), reads=[t_mod, t_norms], writes=[t_G])
                    S.op("dve", lambda e, s=s, Gt=Gt: e.tensor_scalar(
                        out=Gt[:, :, s], in0=Gt[:, :, s], scalar1=float(np.sqrt(D)), scalar2=None, op0=ALU.mult),
                        reads=[t_G], writes=[t_G])
            S.op("dve", lambda e: e.tensor_scalar(out=gains[:, 1:2], in0=gains[:, 1:2], scalar1=8.0, scalar2=None,
                                                  op0=ALU.mult), reads=[t_gains], writes=[t_gains])
            S.op("dve", lambda e: e.tensor_scalar(out=gains[:, 3:4], in0=gains[:, 3:4], scalar1=8.0, scalar2=None,
                                                  op0=ALU.mult), reads=[t_gains], writes=[t_gains])
            stg = F0["stage"]; tstg = tn("stage")
            for half in range(7):
                S.dma("sp", lambda e, half=half, Wl=Wl: e.dma_start(
                    out=stg[:, :], in_=Wl["rpbg"][:, half * 1024:(half + 1) * 1024]), tstg, writes=[tstg])
                S.op("act", lambda e: e.activation(out=stg[:, :], in_=stg[:, :], func=AF.Exp), reads=[tstg], writes=[tstg])
                S.op("dve", lambda e, half=half: e.tensor_tensor(
                    out=TBf[:, half * 2:(half + 1) * 2, :, :].rearrange("p f h c -> p (f h) c"),
                    in0=stg[:, :].rearrange("p (a c) -> p a c", c=64),
                    in1=cmask[:, :].rearrange("p (a c) -> p a c", a=1).broadcast_to([128, 16, 64]), op=ALU.mult),
                    reads=[tstg, t_cm], writes=[t_tb])
            def specials(e):
                e.memset(TBs4[:, 0, :, :], 0.0)
                e.tensor_copy(out=TBs4[:, 1, :, :], in_=TBf[:, 3, :, :])
                e.tensor_copy(out=TBm4[:, :, :, :], in_=TBf[:, 10:12, :, :])
                e.memset(TBs4[64:128, 1, :, :], 0.0)
                return e.memset(TBm4[0:64, 1, :, :], 0.0)
            S.op("pool", specials, reads=[t_tb], writes=[t_tb])

            if STOP < 1:
                break
            S.barrier()
            hT = PA["hT"]; sqb = PA["sqb"]
            S.op("dve", lambda e: e.memset(Va[:, :, :, 64:65], 1.0), writes=[t_va])
            S.op("dve", lambda e: e.memset(Vb[:, :, 64:65], 1.0), writes=[t_vb])
            for cc in range(4):
                S.dma("pool", lambda e, Wl=Wl, cc=cc: e.dma_start(
                    out=PA["Wk"][:, cc, :, :],
                    in_=Wl["win"][:, 512 + cc * 128:512 + (cc + 1) * 128].rearrange("(k p) n -> p k n", p=128)),
                    tn("Wk"), writes=[tn("Wk")])
            S.dma("pool", lambda e, Wl=Wl: e.dma_start(
                out=PA["Wk"][:, 4, :, :], in_=Wl["win"][:, 2048:2176].rearrange("(k p) n -> p k n", p=128)),
                tn("Wk"), writes=[tn("Wk")])
            S.dma("pool", lambda e, Wl=Wl: e.dma_start(
                out=PA["Wv"][:, :, 0:512], in_=Wl["win"][:, 1024:1536].rearrange("(k p) n -> p k n", p=128)),
                tn("Wv"), writes=[tn("Wv")])
            S.dma("pool", lambda e, Wl=Wl: e.dma_start(
                out=PA["Wv"][:, :, 512:640], in_=Wl["win"][:, 2176:2304].rearrange("(k p) n -> p k n", p=128)),
                tn("Wv"), writes=[tn("Wv")])
            hT2 = [PA["hT"], PA["hT1"]]
            RS = [dict(ssb=banks[3], tss=tbank[3], id="_a0", rs2=FAB["rs2"]),
                  dict(ssb=banks[4], tss=tbank[4], id="_a1", rs2=FAB["RD0"]),
                  dict(ssb=banks[7], tss=tbank[7], id="_a2", rs2=FAB["RD1"])]
            sq2s = [PA["sq2"], PA["sq2b"], PA["sq2c"]]

            def a_norm(tb):
                s = 0 if tb < 8 else 1
                blk = tb if tb < 8 else 0
                norm_mod(s, blk, G1, 0, hT2[tb % 2], sqb, FAB)

            a_norm(0)
            for tb in range(9):
                s = 0 if tb < 8 else 1
                blk = tb if tb < 8 else 0
                k0 = tb * 256
                hT = hT2[tb % 2]
                for pr in range(2):
                    pb = banks[1 + pr]; tpb = tbank[1 + pr]
                    def mm(e, pr=pr, pb=pb, hT=hT):
                        for j in range(2):
                            for k in range(8):
                                i = e.matmul(pb[:, j * 256:(j + 1) * 256], lhsT=PA["Wk"][:, 2 * pr + j, k, :], rhs=hT[:, k, :],
                                             start=(k == 0), stop=(k == 7))
                        return i
                    S.op("pe", mm, reads=[tn("Wk"), tn("hT")], writes=[tpb])
                def mmb(e, hT=hT):
                    for k in range(8):
                        i = e.matmul(banks[0][:, 256:512], lhsT=PA["Wk"][:, 4, k, :], rhs=hT[:, k, :], start=(k == 0), stop=(k == 7))
                    return i
                S.op("pe", mmb, reads=[tn("Wk"), tn("hT")], writes=[tbank[0]])
                for tt in range(2):
                    def mmv(e, tt=tt, hT=hT):
                        for k in range(8):
                            e.matmul(banks[5 + tt][:, 0:512], lhsT=hT[:, k, tt * 128:(tt + 1) * 128], rhs=PA["Wv"][:, k, 0:512],
                                     start=(k == 0), stop=(k == 7))
                        return e
                    def mmv2(e, tt=tt, hT=hT):
                        for k in range(8):
                            i = e.matmul(banks[7][:, 256 + tt * 128:256 + (tt + 1) * 128], lhsT=hT[:, k, tt * 128:(tt + 1) * 128],
                                         rhs=PA["Wv"][:, k, 512:640], start=(k == 0), stop=(k == 7))
                        return i
                    def mmva(e, tt=tt, hT=hT):
                        for k in range(8):
                            i = e.matmul(banks[5 + tt][:, 0:512], lhsT=hT[:, k, tt * 128:(tt + 1) * 128], rhs=PA["Wv"][:, k, 0:512],
                                         start=(k == 0), stop=(k == 7))
                        return i
                    S.op("pe", mmva, reads=[tn("Wv"), tn("hT")], writes=[tbank[5 + tt]])
                    S.op("pe", mmv2, reads=[tn("Wv"), tn("hT")], writes=[tbank[7]])
                streams = []
                for pr in range(2):
                    streams.append(S.capture(lambda pr=pr: qk_norm(
                        banks[1 + pr][:, 0:512], tbank[1 + pr], 1, KaT[:, 2 * pr:2 * pr + 2, k0:k0 + 256], [t_ka], 512,
                        sq2=sq2s[pr], R=RS[pr])))
                streams.append(S.capture(lambda: qk_norm(
                    banks[0][:, 256:512], tbank[0], 3, KbT[:, k0:k0 + 256], [t_kb], 256,
                    rope_blk=(blk if s == 0 else None), sq2=sq2s[2], hat=PA["khat"], R=RS[2])))
                def vev():
                    for tt in range(2):
                        kt = tb * 2 + tt
                        pv = banks[5 + tt][:, 0:512].rearrange("p (c e d) -> p c e d", c=4, e=2)
                        S.op("dve", lambda e, kt=kt, pv=pv: e.tensor_copy(out=Va[:, kt, :, 0:64], in_=pv[:, :, 0, :]),
                             reads=[tbank[5 + tt]], writes=[t_va])
                        S.op("dve", lambda e, kt=kt, pv=pv: e.tensor_copy(out=Va[:, kt, :, 65:129], in_=pv[:, :, 1, :]),
                             reads=[tbank[5 + tt]], writes=[t_va])
                        vb_ps = banks[7][:, 256 + tt * 128:256 + (tt + 1) * 128]
                        S.op("act", lambda e, kt=kt, vb_ps=vb_ps: e.activation(out=Vb[:, kt, 0:64], in_=vb_ps[:, 0:64], func=AF.Copy),
                             reads=[tbank[7]], writes=[t_vb])
                        S.op("act", lambda e, kt=kt, vb_ps=vb_ps: e.activation(out=Vb[:, kt, 65:129], in_=vb_ps[:, 64:128], func=AF.Copy),
                             reads=[tbank[7]], writes=[t_vb])
                streams.append(S.capture(vev))
                if tb + 1 < 9:
                    streams.append(S.capture(lambda: a_norm(tb + 1)))
                while any(streams):
                    for st_ in streams:
                        if st_:
                            S.play(st_.pop(0))

            if STOP < 2:
                break
            S.barrier()
            hT = PB["hT"]; sqb = PB["sqb"]
            cur = {"oT": PB["oT0"], "toT": tn("oT0")}
            delayed = []
            nqb = min(KNQB, 8 if last else 9)
            Qa2 = [PB["Qa0"], PB["Qa1"]]; Qb2 = [PB["Qb0"], PB["Qb1"]]

            def load_w(src_ap, pool="q", Wl=Wl):
                nm = wq_bufs[rr["wq"] % 4]; rr["wq"] += 1
                wbuf = PB[nm]
                S.dma("pool", lambda e, wbuf=wbuf, src_ap=src_ap: e.dma_start(
                    out=wbuf[:, :, :], in_=src_ap.rearrange("(k p) n -> p k n", p=128)), tn(nm), writes=[tn(nm)])
                return wbuf, tn(nm)

            def finalize(chunk, accb, taccb):
                i = rr["fin"] % 2; rr["fin"] += 1
                for d in [d for d in delayed if d[2] == i]:
                    delayed.remove(d); d[1]()
                RD = FAB["RD%d" % i]; RB = FAB["RB%d" % i]; OU = FAB["OU%d" % i]
                tRD = tn("RD%d" % i); tRB = tn("RB%d" % i); tOU = tn("OU%d" % i); tscr = tn("scr%d" % i)
                oT = cur["oT"]; toT = cur["toT"]
                S.op("act", lambda e: e.activation(out=RD[64:65, 0:256], in_=accb[64:65, 0:256], func=AF.Ln), reads=[taccb], writes=[tRD])
                S.op("act", lambda e: e.activation(out=RD[32:64, 256:512], in_=accb[32:64, 256:512], func=AF.Ln), reads=[taccb], writes=[tRD])
                S.op("act", lambda e: e.activation(out=RD[64:65, 0:256], in_=RD[64:65, 0:256], func=AF.Exp, scale=-1.0), reads=[tRD], writes=[tRD])
                S.op("act", lambda e: e.activation(out=RD[32:64, 256:512], in_=RD[32:64, 256:512], func=AF.Exp, scale=-1.0), reads=[tRD], writes=[tRD])
                S.op("dve", lambda e: e.tensor_copy(out=OU[0:64, :], in_=accb[0:64, 0:256]), reads=[taccb], writes=[tOU])
                S.op("dve", lambda e: e.tensor_copy(out=OU[64:128, :], in_=accb[64:128, 256:512]), reads=[taccb], writes=[tOU])
                S.dma("sp", lambda e: e.dma_start(out=scr[i, 1:2, :], in_=RD[64:65, 0:256]), tscr, reads=[tRD], writes=[tscr])
                S.dma("sp", lambda e: e.dma_start(out=scr[i, 0:1, :], in_=RD[63:64, 256:512]), tscr, reads=[tRD], writes=[tscr])
                S.dma("sp", lambda e: e.dma_start(out=RB[0:64, :], in_=scr[i, 1:2, :].partition_broadcast(64)), tRB,
                      reads=[tscr], writes=[tRB])
                S.dma("sp", lambda e: e.dma_start(out=RB[64:128, :], in_=scr[i, 0:1, :].partition_broadcast(64)), tRB,
                      reads=[tscr], writes=[tRB])
                def part2():
                    S.op("dve", lambda e: e.tensor_tensor(out=oT[:, chunk, :], in0=OU[:, :], in1=RB[:, :], op=ALU.mult),
                         reads=[tOU, tRB], writes=[toT])
                delayed.append([8, part2, i])

            def st_banks():
                bi = rr["st"] % 2; rr["st"] += 1
                ix, iy = (2, 4)[bi], (3, 7)[bi]
                return (banks[ix], tbank[ix], banks[iy], tbank[iy], PB["pt%d" % (2 * bi)], tn("pt%d" % (2 * bi)),
                        PB["pt%d" % (2 * bi + 1)], tn("pt%d" % (2 * bi + 1)))

            pend = []
            hops = []

            hopsB = []

            def play_hops(n):
                if hops:
                    for _ in range(n):
                        if hops:
                            S.play(hops.pop(0))
                    return
                for _ in range(cur.get("nb", n)):
                    if hopsB:
                        S.play(hopsB.pop(0))

            def emit_step(s1_pe, s1_rest, s2_pe, s2_post, nh=1):
                s1_pe()
                prev = pend.pop() if pend else None
                if prev:
                    prev[0]()
                s1_rest()
                if prev and prev[1]:
                    prev[1]()
                pend.append((s2_pe, s2_post))
                for d in list(delayed):
                    d[0] -= 1
                    if d[0] <= 0:
                        delayed.remove(d); d[1]()
                play_hops(cur.get("nh", nh))

            def flush_steps():
                if pend:
                    p = pend.pop()
                    p[0]()
                    if p[1]:
                        p[1]()
                for d in list(delayed):
                    delayed.remove(d); d[1]()

            def dense_attn(KT_ap, tK, Vfn, tV, Qbuf, c, tQ, ktiles, chunk):
                ai = 5 + rr["acc"] % 2; rr["acc"] += 1
                accb = banks[ai]; taccb = tbank[ai]
                npair = len(ktiles) // 2
                for n_i in range(npair):
                    kt0, kt1 = ktiles[2 * n_i], ktiles[2 * n_i + 1]
                    X, tX, Y, tY, pX, tpX, pY, tpY = st_banks()
                    def mmqk(e, kt0=kt0, kt1=kt1, X=X, Y=Y):
                        e.matmul(X[:, 0:256], lhsT=KT_ap(0, kt0), rhs=Qbuf[0:64, c, :], start=True, stop=True)
                        e.matmul(Y[:, 0:256], lhsT=KT_ap(1, kt0), rhs=Qbuf[64:128, c, :], start=True, stop=True)
                        e.matmul(X[:, 256:512], lhsT=KT_ap(0, kt1), rhs=Qbuf[0:64, c, :], start=True, stop=True)
                        return e.matmul(Y[:, 256:512], lhsT=KT_ap(1, kt1), rhs=Qbuf[64:128, c, :], start=True, stop=True)
                    first = n_i == 0; lastk = n_i == npair - 1
                    def mmpv(e, kt0=kt0, kt1=kt1, pX=pX, pY=pY, first=first, lastk=lastk):
                        e.matmul(accb[:, 0:256], lhsT=Vfn(kt0, 0), rhs=pX[:, 0:256], start=first, stop=False)
                        e.matmul(accb[:, 0:256], lhsT=Vfn(kt1, 0), rhs=pX[:, 256:512], start=False, stop=lastk)
                        e.matmul(accb[:, 256:512], lhsT=Vfn(kt0, 1), rhs=pY[:, 0:256], start=False, stop=False,
                                 skip_group_check=True)
                        return e.matmul(accb[:, 256:512], lhsT=Vfn(kt1, 1), rhs=pY[:, 256:512], start=False, stop=lastk,
                                        skip_group_check=True)
                    def s1_pe(mmqk=mmqk, tX=tX, tY=tY):
                        S.op("pe", mmqk, reads=[tK, tQ], writes=[tX, tY])
                    def s1_rest(X=X, Y=Y, pX=pX, pY=pY, tX=tX, tY=tY, tpX=tpX, tpY=tpY):
                        S.op("act", lambda e: e.activation(out=pX[:, :], in_=X[:, :], func=AF.Exp), reads=[tX], writes=[tpX])
                        S.op("act", lambda e: e.activation(out=pY[:, :], in_=Y[:, :], func=AF.Exp), reads=[tY], writes=[tpY])
                    def s2_pe(mmpv=mmpv, tpX=tpX, tpY=tpY):
                        S.op("pe", mmpv, reads=[tV, tpX, tpY], writes=[taccb])
                    s2_post = (lambda: finalize(chunk, accb, taccb)) if lastk else None
                    emit_step(s1_pe, s1_rest, s2_pe, s2_post)

            KaT_ap = lambda c: (lambda e_, kt: KaT[e_ * 64:(e_ + 1) * 64, c, kt * 128:(kt + 1) * 128])
            Va_fn = lambda c: (lambda kt, e_: (Va[:, kt, c, 0:128] if e_ == 0 else Va[:, kt, c, 1:129]))
            KbT_ap = lambda e_, kt: KbT[e_ * 64:(e_ + 1) * 64, kt * 128:(kt + 1) * 128]
            Vb_fn = lambda kt, e_: (Vb[:, kt, 0:128] if e_ == 0 else Vb[:, kt, 1:129])

            def na_unit(qb, hg, Qa_):
                tQa = tn("Qa%d" % (qb % 2))
                accs = [banks[5], banks[6]]; taccs = [tbank[5], tbank[6]]
                for sub in range(2):
                    r = qb * 4 + sub * 2
                    if r <= 2:
                        tiles = [(kr, "f") for kr in (0, 2, 4, 6)]
                    elif r >= 28:
                        tiles = [(kr, "f") for kr in (24, 26, 28, 30)]
                    else:
                        tiles = [(r + dl, "m") for dl in (-4, -2, 0, 2, 4)]
                    tiles = tiles + [(32, "c"), (34, "c")]
                    for n_i, (kr, kind) in enumerate(tiles):
                        X, tX, Y, tY, pX, tpX, pY, tpY = st_banks()
                        def mmqk(e, kr=kr, X=X, Y=Y, sub=sub):
                            for cl in range(2):
                                c = 2 * hg + cl
                                e.matmul(X[:, cl * 128:(cl + 1) * 128], lhsT=KaT[0:64, c, kr * 64:kr * 64 + 128],
                                         rhs=Qa_[0:64, c, sub * 128:(sub + 1) * 128], start=True, stop=True)
                                i = e.matmul(Y[:, cl * 128:(cl + 1) * 128], lhsT=KaT[64:128, c, kr * 64:kr * 64 + 128],
                                             rhs=Qa_[64:128, c, sub * 128:(sub + 1) * 128], start=True, stop=True)
                            return i
                        def s1_pe(mmqk=mmqk, tX=tX, tY=tY):
                            S.op("pe", mmqk, reads=[t_ka, tQa], writes=[tX, tY])
                        def s1_rest(kind=kind, kr=kr, r=r, X=X, Y=Y, pX=pX, pY=pY, tX=tX, tY=tY, tpX=tpX, tpY=tpY):
                            if kind == "c":
                                S.op("act", lambda e: e.activation(out=pX[:, 0:256], in_=X[:, 0:256], func=AF.Exp),
                                     reads=[tX], writes=[tpX])
                                S.op("act", lambda e: e.activation(out=pY[:, 0:256], in_=Y[:, 0:256], func=AF.Exp),
                                     reads=[tY], writes=[tpY])
                                return
                            dl = kr - r
                            if kind == "m" and dl == 4:
                                tbase, f0 = TBs4, 0
                            elif kind == "m" and dl == -4:
                                tbase, f0 = TBm4, 0
                            else:
                                tbase, f0 = TBf, 6 - dl
                            tv = tbase[:].rearrange("p f (c e) q -> p f c e q", e=2)
                            for e_, (B_, tB_, p_, tp_) in enumerate(((X, tX, pX, tpX), (Y, tY, pY, tpY))):
                                ex = PB["ex%d" % e_]; tex = tn("ex%d" % e_)
                                S.op("act", lambda e, B_=B_, ex=ex: e.activation(out=ex[:, 0:256], in_=B_[:, 0:256], func=AF.Exp),
                                     reads=[tB_], writes=[tex])
                                tab = tv[:, f0:f0 + 2, 2 * hg:2 * hg + 2, e_, :].rearrange("p f c q -> p c f q")
                                S.op("dve", lambda e, ex=ex, p_=p_, tab=tab: e.tensor_tensor(
                                    out=p_[:, 0:256].rearrange("p (c f q) -> p c f q", c=2, f=2),
                                    in0=ex[:, 0:256].rearrange("p (c f q) -> p c f q", c=2, f=2),
                                    in1=tab, op=ALU.mult), reads=[tex, t_tb], writes=[tp_])
                        first = n_i == 0; lastk = n_i == len(tiles) - 1
                        def mmpv(e, kr=kr, pX=pX, pY=pY, first=first, lastk=lastk, sub=sub):
                            for cl in range(2):
                                c = 2 * hg + cl
                                acc = accs[cl]
                                e.matmul(acc[:, sub * 128:(sub + 1) * 128], lhsT=Va[:, kr // 2, c, 0:128],
                                         rhs=pX[:, cl * 128:(cl + 1) * 128], start=first, stop=lastk)
                                i = e.matmul(acc[:, 256 + sub * 128:256 + (sub + 1) * 128], lhsT=Va[:, kr // 2, c, 1:129],
                                             rhs=pY[:, cl * 128:(cl + 1) * 128], start=False, stop=lastk,
                                             skip_group_check=True)
                            return i
                        def s2_pe(mmpv=mmpv, tpX=tpX, tpY=tpY):
                            S.op("pe", mmpv, reads=[t_va, tpX, tpY], writes=taccs)
                        s2_post = None
                        if lastk and sub == 1:
                            s2_post = lambda: [finalize(2 * hg + cl, accs[cl], taccs[cl]) for cl in range(2)]
                        emit_step(s1_pe, s1_rest, s2_pe, s2_post)

            def q_units(qb):
                s = 0 if qb < 8 else 1
                blk = qb if qb < 8 else 0
                par = qb % 2
                units = [lambda: norm_mod(s, blk, G1, 0, hT, sqb, FAB)]
                for pr in range(4):
                    def u(pr=pr):
                        isb = pr >= 2
                        col0 = pr * 256 if not isb else 1536 + (pr - 2) * 256
                        wbufs = [load_w(Wl["win"][:, col0 + j * 128:col0 + (j + 1) * 128]) for j in range(2)]
                        pb = banks[1]; tpb = tbank[1]
                        def mm(e):
                            for j in range(2):
                                for k in range(8):
                                    i = e.matmul(pb[:, j * 256:(j + 1) * 256], lhsT=wbufs[j][0][:, k, :], rhs=hT[:, k, :],
                                                 start=(k == 0), stop=(k == 7))
                            return i
                        S.op("pe", mm, reads=[wbufs[0][1], wbufs[1][1], tn("hT")], writes=[tpb])
                        if not isb:
                            qk_norm(pb[:, 0:512], tpb, 0, Qa2[par][:, 2 * pr:2 * pr + 2, :], [tn("Qa%d" % par)], 512,
                                    sq2=PB["sq2"])
                        else:
                            qk_norm(pb[:, 0:512], tpb, 2, Qb2[par][:, 2 * (pr - 2):2 * (pr - 2) + 2, :], [tn("Qb%d" % par)],
                                    512, rope_blk=(blk if s == 0 else None), sq2=PB["sq2"], hat=PB["qhat"])
                    units.append(u)
                return units

            def att_units(qb):
                par = qb % 2
                Qa_ = Qa2[par]; Qb_ = Qb2[par]; tQa = tn("Qa%d" % par); tQb = tn("Qb%d" % par)
                if qb == 8:
                    ua = [(lambda c=c: dense_attn(KaT_ap(c), t_ka, Va_fn(c), t_va, Qa_, c, tQa, [16, 17], c)) for c in range(4)]
                    ub = [(lambda c=c: dense_attn(KbT_ap, t_kb, Vb_fn, t_vb, Qb_, c, tQb, [16, 17], 4 + c)) for c in range(4)]
                    return ua + ub
                ug = [(lambda c=c: dense_attn(KbT_ap, t_kb, Vb_fn, t_vb, Qb_, c, tQb, list(range(18)), 4 + c)) for c in range(4)]
                un = [(lambda hg=hg: na_unit(qb, hg, Qa_)) for hg in range(2)]
                return [ug[0], un[0], ug[1], ug[2], un[1], ug[3]]

            def wout_unit(qb):
                s = 0 if qb < 8 else 1
                blk = qb if qb < 8 else 0
                oT = PB["oT%d" % (qb % 2)]; toT = tn("oT%d" % (qb % 2))
                st_t = stream_tiles(s, blk)
                for oc in range(8):
                    wbuf, twb = load_w(Wl["wout"][:, oc * 128:(oc + 1) * 128], pool="o")
                    pb = banks[1][:, (oc % 2) * 256:(oc % 2) * 256 + 256]; tpb = tbank[1]
                    def mm(e, wbuf=wbuf, pb=pb):
                        for k in range(8):
                            i = e.matmul(pb, lhsT=wbuf[:, k, :], rhs=oT[:, k, :], start=(k == 0), stop=(k == 7))
                        return i
                    S.op("pe", mm, reads=[twb, toT], writes=[tpb])
                    S.op("dve", lambda e, oc=oc, pb=pb: e.scalar_tensor_tensor(
                        out=stream_ap(s, oc, blk), in0=pb, scalar=modT[:, 16 + oc, s:s + 1],
                        in1=stream_ap(s, oc, blk), op0=ALU.mult, op1=ALU.add),
                        reads=[tpb, t_mod, st_t[oc]], writes=[st_t[oc]])

            for u in q_units(0):
                u()
            for qb in range(nqb):
                cur["oT"] = PB["oT%d" % (qb % 2)]; cur["toT"] = tn("oT%d" % (qb % 2))
                wl = S.capture(lambda: wout_unit(qb - 1)) if qb >= 1 else []
                qu = q_units(qb + 1) if qb + 1 < nqb else []
                qn = S.capture(qu[0]) if qu else []
                qp = S.capture(lambda: [u() for u in qu[1:]]) if qu else []
                while wl or qn:
                    if wl:
                        hops.append(wl.pop(0))
                    if qn:
                        hops.append(qn.pop(0))
                hopsB.extend(qp)
                nsteps = 64 if qb < 8 else 8
                if qb < 8:
                    cur["nh"] = 2
                    left = max(1, nsteps - (len(hops) + 1) // 2)
                    cur["nb"] = -(-len(hopsB) // left)
                else:
                    cur["nh"] = -(-len(hops) // nsteps)
                    cur["nb"] = -(-len(hopsB) // nsteps) if hopsB else 1
                for u in att_units(qb):
                    u()
                flush_steps()
                while hops:
                    S.play(hops.pop(0))
                while hopsB:
                    S.play(hopsB.pop(0))
            wout_unit(nqb - 1)

            if STOP < 3:
                break
            S.barrier()
            h2s = [PC["h2"], PC["h2b"]]; th2 = [tn("h2_0"), tn("h2_1")]
            sqb = PC["sqb"]; act = PC["act"]
            units = [(0, b) for b in range(8)] + ([] if last else [(1, 0)])
            fblocks = [units[i:i + 3] for i in range(0, len(units), 3)]
            def c_norm(ui):
                for j, (s, b) in enumerate(fblocks[ui]):
                    norm_mod(s, b, G2, 24, h2s[ui % 2], sqb, FC, ncols_off=j * 256, thT=th2[ui % 2],
                             ssr=(banks[1][:, 256:512], tbank[1]))
            c_norm(0)
            for ui, ublk in enumerate(fblocks):
                nb = len(ublk)
                halves = [(0, 512), (512, 256)] if nb == 3 else [(0, 512)]
                h2 = h2s[ui % 2]; th2_ = th2[ui % 2]
                nhops = S.capture(lambda: c_norm(ui + 1)) if ui + 1 < len(fblocks) else []
                per_f = -(-len(nhops) // 20)
                for f in range(22):
                    for _ in range(per_f):
                        if nhops:
                            S.play(nhops.pop(0))
                    wg = PC["Wg%d" % (f % 2)]; twg = tn("Wg%d" % (f % 2))
                    wu = PC["Wu%d" % (f % 2)]; twu = tn("Wu%d" % (f % 2))
                    S.dma("pool", lambda e, f=f, wg=wg, Wl=Wl: e.dma_start(
                        out=wg[:, :, :], in_=Wl["wg"][:, f * 128:(f + 1) * 128].rearrange("(k p) n -> p k n", p=128)),
                        twg, writes=[twg])
                    S.dma("pool", lambda e, f=f, wu=wu, Wl=Wl: e.dma_start(
                        out=wu[:, :, :], in_=Wl["wu"][:, f * 128:(f + 1) * 128].rearrange("(k p) n -> p k n", p=128)),
                        twu, writes=[twu])
                    gset = [banks[(f % 2) * 4 + hh] for hh in range(2)]; tg = [tbank[(f % 2) * 4 + hh] for hh in range(2)]
                    uset = [banks[(f % 2) * 4 + 2 + hh] for hh in range(2)]; tu = [tbank[(f % 2) * 4 + 2 + hh] for hh in range(2)]
                    def mmg(e, wg=wg, gset=gset, halves=halves, h2=h2):
                        for k in range(8):
                            for hh, (c0, w) in enumerate(halves):
                                i = e.matmul(gset[hh][:, 0:w], lhsT=wg[:, k, :], rhs=h2[:, k, c0:c0 + w],
                                             start=(k == 0), stop=(k == 7))
                        return i
                    S.op("pe", mmg, reads=[twg, th2_], writes=tg[:len(halves)])
                    def mmu(e, wu=wu, uset=uset, halves=halves, h2=h2):
                        for k in range(8):
                            for hh, (c0, w) in enumerate(halves):
                                i = e.matmul(uset[hh][:, 0:w], lhsT=wu[:, k, :], rhs=h2[:, k, c0:c0 + w],
                                             start=(k == 0), stop=(k == 7))
                        return i
                    S.op("pe", mmu, reads=[twu, th2_], writes=tu[:len(halves)])
                    for hh, (c0, w) in enumerate(halves):
                        sg = FC["sg%d" % hh]; tsg = tn("sg%d" % hh)
                        S.op("act", lambda e, sg=sg, g=gset[hh], w=w: e.activation(out=sg[:, 0:w], in_=g[:, 0:w], func=AF.Silu),
                             reads=[tg[hh]], writes=[tsg])
                        S.op("dve", lambda e, sg=sg, u=uset[hh], f=f, c0=c0, w=w: e.tensor_tensor(
                            out=act[:, f, c0:c0 + w], in0=u[:, 0:w], in1=sg[:, 0:w], op=ALU.mult),
                            reads=[tu[hh], tsg], writes=[tn("act")])
                while nhops:
                    S.play(nhops.pop(0))
                for oc in range(8):
                    wd = PC["Wd%d" % (oc % 2)]; twd = tn("Wd%d" % (oc % 2))
                    S.dma("pool", lambda e, oc=oc, wd=wd, Wl=Wl: e.dma_start(
                        out=wd[:, :, :], in_=Wl["wd"][:, oc * 128:(oc + 1) * 128].rearrange("(f p) n -> p f n", p=128)),
                        twd, writes=[twd])
                    yset = [banks[(oc % 2) * 2 + hh] for hh in range(2)]; ty = [tbank[(oc % 2) * 2 + hh] for hh in range(2)]
                    def mmd(e, wd=wd, yset=yset, halves=halves):
                        for f in range(22):
                            for hh, (c0, w) in enumerate(halves):
                                i = e.matmul(yset[hh][:, 0:w], lhsT=wd[:, f, :], rhs=act[:, f, c0:c0 + w],
                                             start=(f == 0), stop=(f == 21))
                        return i
                    S.op("pe", mmd, reads=[twd, tn("act")], writes=ty[:len(halves)])
                    for hh, (c0, w) in enumerate(halves):
                        for jj in range(w // 256):
                            s, blk = ublk[(c0 + jj * 256) // 256]
                            S.op("dve", lambda e, oc=oc, y=yset[hh], s=s, blk=blk, jj=jj: e.scalar_tensor_tensor(
                                out=stream_ap(s, oc, blk), in0=y[:, jj * 256:(jj + 1) * 256], scalar=modT[:, 40 + oc, s:s + 1],
                                in1=stream_ap(s, oc, blk), op0=ALU.mult, op1=ALU.add),
                                reads=[ty[hh], t_mod, stream_tiles(s, blk)[oc]], writes=[stream_tiles(s, blk)[oc]])

        S.barrier()
        S.dma("sp", lambda e: e.dma_start(out=xT_out.rearrange("(k p) n -> p k n", p=128), in_=xT[:]), t_out,
              reads=[t for row in t_x for t in row])
        S.dma("sp", lambda e: e.dma_start(out=cxT_out.rearrange("(k p) n -> p k n", p=128), in_=cT[:]), t_out,
              reads=t_c)
        S.final.append(t_out)
        S.emit()
    return nc


def _const_tables():
    t = np.arange(NT)
    half = 32
    inv = (10000.0 ** (-np.arange(0, half, 2, dtype=np.float32) / half)).astype(np.float32)
    row = (t // 64).astype(np.float32)[:, None] * inv
    col = (t % 64).astype(np.float32)[:, None] * inv
    cr, sr, cc, sc = np.cos(row), np.sin(row), np.cos(col), np.sin(col)
    cos64 = np.concatenate([cr, cr, cc, cc], axis=1).T
    sin64 = np.concatenate([-sr, sr, -sc, sc], axis=1).T
    cosT = np.concatenate([cos64, cos64], axis=0).astype(np.float32)
    sinT = np.concatenate([sin64, sin64], axis=0).astype(np.float32)
    pi = np.zeros(64, dtype=np.int64)
    for i in range(64):
        base = (i // 32) * 32; j = i % 32
        pi[i] = base + (j + 16) % 32
    perm = np.zeros((128, 128), dtype=np.float32)
    for hh in range(2):
        for i in range(64):
            perm[hh * 64 + pi[i], hh * 64 + i] = 1.0
    cidx = np.arange(64)
    cs = np.clip(cidx - 8, 0, 48)
    col_ok = (cidx[None, :] >= cs[:, None]) & (cidx[None, :] < cs[:, None] + 16)
    cm = col_ok.T.astype(np.float32)
    cmask = np.concatenate([cm, cm], axis=0)
    return np.ascontiguousarray(cosT), np.ascontiguousarray(sinT), perm, np.ascontiguousarray(cmask)


def _layer_arrays(l, w_ada, b_ada, attn_norm, w_in, q_norm_a, k_norm_a, q_norm_b, k_norm_b, rpb, w_out, ffn_norm,
                  w_gate, w_up, w_down):
    fm = lambda v: np.ascontiguousarray(v.reshape(-1, 128).T)
    hb = np.array([c + 4 * e for c in range(4) for e in range(2)])
    colperm = np.arange(2304)
    colperm[1536:2048] = (1536 + hb[:, None] * 64 + np.arange(64)[None, :]).reshape(-1)
    rowperm = np.arange(1024)
    rowperm[512:1024] = (512 + hb[:, None] * 64 + np.arange(64)[None, :]).reshape(-1)
    cidx = np.arange(64)
    dc = np.clip(cidx[:, None] - cidx[None, :], -15, 15) + 15
    g = np.empty((2, 64, NFI, 8, 64), dtype=np.float32)
    for j in range(2):
        for fi in range(NFI):
            dr = 13 + j - fi
            g[j, :, fi, :, :] = np.transpose(rpb[l][:, dr, :][:, dc], (1, 0, 2))
    gains = np.stack([np.tile(q_norm_a[l], 2), np.tile(k_norm_a[l], 2), np.tile(q_norm_b[l], 2), np.tile(k_norm_b[l], 2)], axis=1)
    return dict(
        wada=w_ada[l], bada=fm(b_ada[l]),
        norms=np.ascontiguousarray(np.concatenate([fm(attn_norm[l]), fm(ffn_norm[l])], axis=1)),
        gains=np.ascontiguousarray(gains.astype(np.float32)),
        win=np.ascontiguousarray(w_in[l][:, colperm]), wout=np.ascontiguousarray(w_out[l][rowperm, :]),
        wg=w_gate[l], wu=w_up[l], wd=w_down[l], rpbg=np.ascontiguousarray(g.reshape(128, NFI * 512)))


_PROGS = {}


def _get_prog(nl, last_flags):
    key = (nl, tuple(last_flags))
    if key not in _PROGS:
        _PROGS[key] = build_program(list(range(nl)), list(last_flags))
    return _PROGS[key]


FUSED_LAYERS = 4


def kernel(x, c, ctx, c_ctx, w_ada, b_ada, attn_norm, w_in, q_norm_a, k_norm_a, q_norm_b, k_norm_b, rpb, w_out,
           ffn_norm, w_gate, w_up, w_down):
    f32 = lambda a: np.asarray(a, dtype=np.float32)
    x, c, ctx, c_ctx = f32(x), f32(c), f32(ctx), f32(c_ctx)
    ws = [f32(a) for a in (w_ada, b_ada, attn_norm, w_in, q_norm_a, k_norm_a, q_norm_b, k_norm_b, rpb, w_out,
                           ffn_norm, w_gate, w_up, w_down)]
    B = x.shape[0]
    cosT, sinT, perm, cmask = _const_tables()
    LA = [_layer_arrays(l, *ws) for l in range(DEPTH)]
    xT = [np.ascontiguousarray(x[b].T) for b in range(B)]
    cxT = [np.ascontiguousarray(ctx[b].T) for b in range(B)]
    conds = []
    for b in range(B):
        cd = np.empty((128, 8, 2), dtype=np.float32)
        cd[:, :, 0] = c[b].reshape(8, 128).T
        cd[:, :, 1] = c_ctx.reshape(8, 128).T
        conds.append(np.ascontiguousarray(cd.reshape(128, 16)))
    l = 0
    while l < DEPTH:
        nl = min(FUSED_LAYERS, DEPTH - l)
        flags = [(l + i) == DEPTH - 1 for i in range(nl)]
        nc = _get_prog(nl, flags)
        in_maps = []
        for b in range(B):
            m = {"xT_in": xT[b], "cxT_in": cxT[b], "cond": conds[b], "cosT": cosT, "sinT": sinT, "cmask": cmask,
                 "perm": perm}
            for i in range(nl):
                for k, v in LA[l + i].items():
                    m["%s%d" % (k, i)] = v
            in_maps.append(m)
        res = run_bass_kernel_spmd(nc, in_maps, core_ids=list(range(B)))
        xT = [np.asarray(res.results[b]["xT_out"]) for b in range(B)]
        cxT = [np.asarray(res.results[b]["cxT_out"]) for b in range(B)]
        l += nl
    out = np.stack([xT[b].T for b in range(B)], axis=0).astype(np.float32)
    return out
```
